# Optimizing a Trainium2 kernel written in Bass

```python
import math
import jax, jax.numpy as jnp
from jax import lax
import numpy as np

D_MODEL = 1024
BATCH = 4
SEQ = 4096
DEPTH = 2

HEAD_DIM = 64
MIX_WIDTH = D_MODEL
FOURIER_WIDTH = MIX_WIDTH // 2
N_HEADS = (MIX_WIDTH - FOURIER_WIDTH) // HEAD_DIM
N_KV_HEADS = 2
KV_GROUP = N_HEADS // N_KV_HEADS
WINDOW = 128
BLOCK = 128
ROPE_THETA = 10000.0
D_FF = 2816
CONV_WIDTH = 3
EPS = 1e-6
Q_COLS = N_HEADS * HEAD_DIM
KV_COLS = N_KV_HEADS * HEAD_DIM
IN_COLS = FOURIER_WIDTH + Q_COLS + 2 * KV_COLS

kernel_name = "hybrid_fourier_swa_convffn_encoder"


def rmsnorm(x, g):
    xf = x.astype(jnp.float32)
    y = xf * lax.rsqrt(jnp.mean(xf * xf, axis=-1, keepdims=True) + EPS)
    return (y * g.astype(jnp.float32)).astype(x.dtype)


def rope(x):
    s, d = x.shape[1], x.shape[-1]
    inv_freq = 1.0 / (ROPE_THETA ** (jnp.arange(0, d, 2, dtype=jnp.float32) / d))
    ang = jnp.arange(s, dtype=jnp.float32)[:, None] * inv_freq[None, :]
    cos = jnp.cos(ang)[None, :, None, :]
    sin = jnp.sin(ang)[None, :, None, :]
    xf = x.astype(jnp.float32)
    x1, x2 = xf[..., : d // 2], xf[..., d // 2:]
    return jnp.concatenate([x1 * cos - x2 * sin, x2 * cos + x1 * sin], axis=-1).astype(x.dtype)


def fourier_mix(u, w_f, b_f):
    f = jnp.fft.fft2(u.astype(jnp.float32), axes=(1, 2), norm="ortho").real.astype(u.dtype)
    return f @ w_f + b_f


def windowed_gqa(q, k, v, sink):
    b, s, _, d = q.shape
    nb = s // BLOCK
    qb = q.reshape(b, nb, BLOCK, N_KV_HEADS, KV_GROUP, d)
    pad = ((0, 0), (BLOCK, BLOCK), (0, 0), (0, 0))
    kp, vp = jnp.pad(k, pad), jnp.pad(v, pad)
    kb = jnp.concatenate([kp[:, i * BLOCK: i * BLOCK + s].reshape(b, nb, BLOCK, N_KV_HEADS, d) for i in range(3)], axis=2)
    vb = jnp.concatenate([vp[:, i * BLOCK: i * BLOCK + s].reshape(b, nb, BLOCK, N_KV_HEADS, d) for i in range(3)], axis=2)
    scores = jnp.einsum("bnqkgd,bnjkd->bnkgqj", qb, kb).astype(jnp.float32) / math.sqrt(d)
    blk = jnp.arange(nb)[:, None]
    qpos = blk * BLOCK + jnp.arange(BLOCK)[None, :]
    kpos = blk * BLOCK - BLOCK + jnp.arange(3 * BLOCK)[None, :]
    valid = (kpos[:, None, :] >= 0) & (kpos[:, None, :] < s) & (jnp.abs(qpos[:, :, None] - kpos[:, None, :]) <= WINDOW)
    scores = jnp.where(valid[None, :, None, None], scores, jnp.finfo(jnp.float32).min)
    sink_col = jnp.broadcast_to(sink.astype(jnp.float32).reshape(1, 1, N_KV_HEADS, KV_GROUP, 1, 1), scores.shape[:-1] + (1,))
    probs = jax.nn.softmax(jnp.concatenate([scores, sink_col], axis=-1), axis=-1)[..., :-1]
    out = jnp.einsum("bnkgqj,bnjkd->bnqkgd", probs.astype(v.dtype), vb)
    return out.reshape(b, s, N_HEADS * d)


def dwconv_centred(h, w, bias):
    hp = jnp.pad(h, ((0, 0), (1, 1), (0, 0)))
    return hp[:, :-2] * w[0] + hp[:, 1:-1] * w[1] + hp[:, 2:] * w[2] + bias


def setup_inputs(seed: int = 0) -> dict:
    key = jax.random.key(seed)
    ks = jax.random.split(key, 16)
    f32 = jnp.float32
    res_scale = (2.0 * DEPTH) ** -0.5

    def nrm(k, shape, scale):
        return jax.random.normal(k, shape, f32) * scale

    return {
        "x": nrm(ks[0], (BATCH, SEQ, D_MODEL), 1.0),
        "norm1": 1.0 + nrm(ks[1], (DEPTH, D_MODEL), 0.02),
        "w_in": nrm(ks[2], (DEPTH, D_MODEL, IN_COLS), D_MODEL ** -0.5),
        "w_fourier": nrm(ks[3], (DEPTH, FOURIER_WIDTH, FOURIER_WIDTH), FOURIER_WIDTH ** -0.5),
        "b_fourier": nrm(ks[4], (DEPTH, FOURIER_WIDTH), 0.02),
        "q_norm": 1.0 + nrm(ks[5], (DEPTH, HEAD_DIM), 0.02),
        "k_norm": 1.0 + nrm(ks[6], (DEPTH, HEAD_DIM), 0.02),
        "sink": nrm(ks[7], (DEPTH, N_HEADS), 0.5),
        "g_fourier_out": 1.0 + nrm(ks[8], (DEPTH, FOURIER_WIDTH), 0.02),
        "g_attn_out": 1.0 + nrm(ks[9], (DEPTH, Q_COLS), 0.02),
        "w_o": nrm(ks[10], (DEPTH, MIX_WIDTH, D_MODEL), MIX_WIDTH ** -0.5 * res_scale),
        "norm2": 1.0 + nrm(ks[11], (DEPTH, D_MODEL), 0.02),
        "w_up": nrm(ks[12], (DEPTH, D_MODEL, 2 * D_FF), D_MODEL ** -0.5),
        "conv_w": nrm(ks[13], (DEPTH, CONV_WIDTH, 2 * D_FF), CONV_WIDTH ** -0.5),
        "conv_b": nrm(ks[14], (DEPTH, 2 * D_FF), 0.02),
        "w_down": nrm(ks[15], (DEPTH, D_FF, D_MODEL), D_FF ** -0.5 * res_scale),
    }


def reference(x, norm1, w_in, w_fourier, b_fourier, q_norm, k_norm, sink, g_fourier_out, g_attn_out, w_o, norm2, w_up, conv_w, conv_b, w_down):
    b, s, _ = x.shape
    for l in range(DEPTH):
        h = rmsnorm(x, norm1[l])
        p = h @ w_in[l]
        u = p[..., :FOURIER_WIDTH]
        q = p[..., FOURIER_WIDTH:FOURIER_WIDTH + Q_COLS].reshape(b, s, N_HEADS, HEAD_DIM)
        k = p[..., FOURIER_WIDTH + Q_COLS:FOURIER_WIDTH + Q_COLS + KV_COLS].reshape(b, s, N_KV_HEADS, HEAD_DIM)
        v = p[..., FOURIER_WIDTH + Q_COLS + KV_COLS:].reshape(b, s, N_KV_HEADS, HEAD_DIM)

        y_f = fourier_mix(u, w_fourier[l], b_fourier[l])

        q = rope(rmsnorm(q, q_norm[l]))
        k = rope(rmsnorm(k, k_norm[l]))
        y_a = windowed_gqa(q, k, v, sink[l])

        mix = jnp.concatenate([rmsnorm(y_f, g_fourier_out[l]), rmsnorm(y_a, g_attn_out[l])], axis=-1)
        x = x + mix @ w_o[l]

        h = rmsnorm(x, norm2[l])
        up = dwconv_centred(h @ w_up[l], conv_w[l], conv_b[l])
        gate, val = up[..., :D_FF], up[..., D_FF:]
        x = x + (jax.nn.silu(gate) * val) @ w_down[l]
    return x
```

```python
import numpy as np
import ml_dtypes
import concourse.bass as bass
import concourse.mybir as mybir
from concourse.bass_utils import run_bass_kernel_spmd

F32, BF16 = mybir.dt.float32, mybir.dt.bfloat16
AF = mybir.ActivationFunctionType
ALU = mybir.AluOpType

D = 1024
T = 2048
SEQ = 4096
DEPTH = 2
DFF = 2816
EPS = 1e-6
NKC = 8
KG = 256
NKG = T // KG
RG = [[0, 1], [2, 3], [4, 5], [6, 7]]


def _rect(ap, whole=False):
    name = ap.tensor.name
    dims = ap.ap
    off = ap.offset
    sp = str(ap.space)
    if "DRAM" in sp.upper() or "HBM" in sp.upper():
        shp = tuple(ap.tensor.shape)
        if len(shp) == 2:
            C = int(shp[1])
            rext = 0
            cext = 0
            for st_, c in dims:
                st_ = abs(int(st_))
                if st_ % C == 0:
                    rext += (c - 1) * (st_ // C)
                else:
                    cext += (c - 1) * st_
            r0, c0 = off // C, off % C
            if c0 + cext < C:
                return (name, r0, r0 + rext + 1, c0, c0 + cext + 1)
        ext = sum((c - 1) * abs(s) for s, c in dims) + 1
        return (name, 0, 1 << 30, off, off + ext)
    if "PSUM" in sp.upper():
        return (name, 0, 128, 0, 1 << 30)
    pstep, pcnt = dims[0]
    if pstep == 0:
        return (name, 0, 128, 0, 1 << 30)
    p0 = off // pstep
    f0 = off % pstep
    ext = sum((c - 1) * abs(s) for s, c in dims[1:]) + 1
    return (name, p0, p0 + pcnt, f0, f0 + ext)


class Prog:
    ENG = ("pe", "act", "dve", "pool", "sp")

    def __init__(self, nc):
        self.nc = nc
        self.ops = []
        self.track = {}

    def add(self, eng, fn, reads=(), writes=(), dma=False, inc=16):
        oid = len(self.ops)
        deps = set()
        rr = [_rect(a) for a in reads]
        wr = [_rect(a) for a in writes]
        wr = wr + [r for r in rr if r[0].startswith("bank")]
        rr = [r for r in rr if not r[0].startswith("bank")]
        for (name, p0, p1, f0, f1) in rr:
            for rec in self.track.get(name, ()):
                if rec[5] and rec[0] < p1 and p0 < rec[1] and rec[2] < f1 and f0 < rec[3]:
                    deps.add(rec[4])
        for (name, p0, p1, f0, f1) in wr:
            lst = self.track.get(name, [])
            keep = []
            for rec in lst:
                if rec[0] < p1 and p0 < rec[1] and rec[2] < f1 and f0 < rec[3]:
                    deps.add(rec[4])
                    if p0 <= rec[0] and rec[1] <= p1 and f0 <= rec[2] and rec[3] <= f1:
                        continue
                keep.append(rec)
            self.track[name] = keep
        for (name, p0, p1, f0, f1) in rr:
            self.track.setdefault(name, []).append((p0, p1, f0, f1, oid, False))
        for (name, p0, p1, f0, f1) in wr:
            self.track.setdefault(name, []).append((p0, p1, f0, f1, oid, True))
        deps.discard(oid)
        self.ops.append(dict(eng=eng, fn=fn, deps=deps, dma=dma, inc=inc, id=oid))
        return oid

    def emit(self, sems):
        ops = self.ops
        has_dep = [False] * len(ops)
        for o in ops:
            for d in o["deps"]:
                has_dep[d] = True
        cnt = {e: 0 for e in self.ENG}
        dcnt = {}
        dq_i = {e: 0 for e in self.ENG}
        last_on_sem = {}
        for o in ops:
            e = o["eng"]
            o["pre"] = None
            if o["dma"]:
                qn = "cc" if o["inc"] != 16 else e
                pool = sems["dma_" + qn]
                s = pool[dq_i.setdefault(qn, 0) % len(pool)]
                dq_i[qn] += 1
                key = id(s)
                o["pre"] = last_on_sem.get(key)
                v = dcnt.get(key, 0) + o["inc"]
                dcnt[key] = v
                o["done"] = (s, v)
                last_on_sem[key] = (s, v)
                o["signal"] = True
            else:
                if has_dep[o["id"]]:
                    cnt[e] += 1
                    o["signal"] = True
                else:
                    o["signal"] = False
                o["done"] = (sems[e], cnt[e])
        per_eng = {e: [] for e in self.ENG}
        waited = {e: {} for e in self.ENG}
        for o in ops:
            e = o["eng"]
            w = {}
            if o["pre"] is not None:
                s, v = o["pre"]
                w[id(s)] = (s, v)
            for d in o["deps"]:
                po = ops[d]
                if (not po["dma"]) and po["eng"] == e and e == "pe":
                    continue
                s, v = po["done"]
                if id(s) not in w or w[id(s)][1] < v:
                    w[id(s)] = (s, v)
            wl = []
            for k, (s, v) in w.items():
                if waited[e].get(k, 0) < v:
                    waited[e][k] = v
                    wl.append((s, v))
            per_eng[e].append((wl, o))
        nc = self.nc
        with nc.Block() as block:
            def run(engine, lst):
                for wl, o in lst:
                    emb = None
                    if wl:
                        emb = wl[-1]
                        wl = wl[:-1]
                    for s, v in wl:
                        engine.wait_ge(s, v)
                    ins = o["fn"](engine)
                    if emb is not None:
                        ins._wait_ge(emb[0], emb[1])
                    if o["signal"]:
                        s, v = o["done"]
                        if o["dma"]:
                            if o["inc"] == 16:
                                ins.then_inc(s, 16)
                            else:
                                ins.then_inc(s)
                        else:
                            ins.then_inc(s, 1)
                return

            @block.tensor
            def _(eng):
                run(eng, per_eng["pe"])

            @block.scalar
            def _(eng):
                run(eng, per_eng["act"])
                for s in sems.get("dma_act", []):
                    v = dcnt.get(id(s), 0)
                    if v:
                        eng.wait_ge(s, v)

            @block.vector
            def _(eng):
                run(eng, per_eng["dve"])

            @block.gpsimd
            def _(eng):
                run(eng, per_eng["pool"])
                for s in sems["dma_pool"] + sems.get("dma_cc", []):
                    v = dcnt.get(id(s), 0)
                    if v:
                        eng.wait_ge(s, v)

            @block.sync
            def _(eng):
                run(eng, per_eng["sp"])
                for s in sems["dma_sp"]:
                    v = dcnt.get(id(s), 0)
                    if v:
                        eng.wait_ge(s, v)


def build_nc(depth=DEPTH, stage="full"):
    nc = bass.Bass("TRN2", target_bir_lowering=False)
    P = Prog(nc)

    def din(name, shape, dt=F32):
        return nc.dram_tensor(name, list(shape), dt, kind="ExternalInput").ap()

    x = din("x", [T, D])
    norm1 = din("norm1", [DEPTH, D]); norm2 = din("norm2", [DEPTH, D])
    w_in = din("w_in", [DEPTH, D, 1280]); w_f = din("w_fourier", [DEPTH, 512, 512])
    b_f = din("b_fourier", [DEPTH, 512]); q_norm = din("q_norm", [DEPTH, 64]); k_norm = din("k_norm", [DEPTH, 64])
    sink = din("sink", [DEPTH, 8]); g_f = din("g_fourier_out", [DEPTH, 512]); g_a = din("g_attn_out", [DEPTH, 512])
    w_o = din("w_o", [DEPTH, D, D]); w_up = din("w_up", [DEPTH, D, 2 * DFF])
    conv_w = din("conv_w", [DEPTH, 3, 2 * DFF]); conv_b = din("conv_b", [DEPTH, 2 * DFF])
    w_down = din("w_down", [DEPTH, DFF, D])
    wbd_d = din("c_wbd", [128, 128], BF16)
    t2_d = din("c_t2", [NKG, 128, 4, 2, 128], BF16)
    ccs_d = din("c_ccs", [128, 2, 4, 512], BF16)
    rope_d = din("c_rope", [128, 2, T], BF16)
    mats_d = din("c_mats", [128, 5, 128], BF16)
    identf_d = din("c_identf", [128, 128], F32)
    masks_d = din("c_masks", [128, 4, 512], BF16)
    onesz_d = din("c_onesz", [128, 2, 128], BF16)
    hmask_d = din("c_hmask", [128, 2], F32)
    y = nc.dram_tensor("y", [T, D], F32, kind="ExternalOutput").ap()

    ub = nc.dram_tensor("ub", [T, 512], BF16).ap()
    ug = nc.dram_tensor("ug", [SEQ, 512], BF16).ap()
    kvb = nc.dram_tensor("kvb", [256, 256], BF16).ap()
    kvg = nc.dram_tensor("kvg", [512, 256], BF16).ap()
    ys = nc.dram_tensor("ys", [32, 65536], BF16).ap()
    hb = nc.dram_tensor("hb", [2, 1024], BF16).ap()
    hg = nc.dram_tensor("hg", [4, 1024], BF16).ap()

    A16 = 47616
    A32 = 6656
    import contextlib
    es = contextlib.ExitStack()
    with es:
        def sb(name, shape, dt):
            return es.enter_context(nc.sbuf_tensor(name, list(shape), dt))
        xT = sb("xT", [128, NKC, T], F32)
        a16 = sb("a16", [128, A16], BF16)
        a32 = sb("a32", [128, A32], F32)
        rope = sb("rope", [128, 2, T], BF16)
        mats = sb("mats", [128, 5, 128], BF16)
        identf = sb("identf", [128, 128], F32)
        masks = sb("masks", [128, 4, 512], BF16)
        onesz = sb("onesz", [128, 2, 128], BF16)
        hmask = sb("hmask", [128, 2], F32)
        epsc = sb("epsc", [128, 1], F32)
        wbd = sb("wbd", [128, 128], BF16)
        g1 = sb("g1", [128, DEPTH, 8], F32); g2 = sb("g2", [128, DEPTH, 8], F32)
        gq = sb("gq", [128, DEPTH], F32); gk = sb("gk", [128, DEPTH], F32)
        esink = sb("esink", [128, DEPTH, 4], F32)
        gF = sb("gF", [128, DEPTH, 4], F32); gA = sb("gA", [128, DEPTH, 4], F32)
        bF = sb("bF", [128, DEPTH, 4], F32)
        cw = sb("cw", [128, DEPTH, 3, 44], F32); cb = sb("cb", [128, DEPTH, 44], F32)
        banks = [es.enter_context(nc.psum_tensor(f"bank{i}", [128, 512], F32)) for i in range(8)]
        sem_c = {e: es.enter_context(nc.semaphore("s_" + e)) for e in ("pe", "act", "dve", "pool", "sp")}
        sem_c["dma_sp"] = [es.enter_context(nc.semaphore(f"dsp{i}")) for i in range(24)]
        sem_c["dma_pool"] = [es.enter_context(nc.semaphore(f"dpl{i}")) for i in range(24)]
        sem_c["dma_cc"] = [es.enter_context(nc.semaphore(f"dcc{i}")) for i in range(3)]
        sem_c["dma_act"] = [es.enter_context(nc.semaphore(f"dac{i}")) for i in range(8)]

        Rm, bones, o1024, o512, identb = (mats[:, i, :] for i in range(5))

        def c16(off, shape):
            n = int(np.prod(shape))
            assert off + n <= A16, (off, n)
            v = a16[:, off:off + n]
            if len(shape) == 2:
                v = v.rearrange("p (a b) -> p a b", a=shape[0], b=shape[1])
            elif len(shape) == 3:
                v = v.rearrange("p (a b c) -> p a b c", a=shape[0], b=shape[1], c=shape[2])
            return v

        def c32(off, shape):
            n = int(np.prod(shape))
            assert off + n <= A32, (off, n)
            v = a32[:, off:off + n]
            if len(shape) == 2:
                v = v.rearrange("p (a b) -> p a b", a=shape[0], b=shape[1])
            return v

        def dma(eng, out, in_, **kw):
            P.add(eng, lambda e: e.dma_start(out=out, in_=in_, **kw), reads=[in_], writes=[out], dma=True)

        def mm(out, lhsT, rhs, start, stop, skip=False):
            P.add("pe", lambda e: e.matmul(out, lhsT, rhs, start=start, stop=stop, skip_group_check=skip),
                  reads=[lhsT, rhs], writes=[out])

        def tr(out, in_, ident):
            P.add("pe", lambda e: e.transpose(out, in_, ident), reads=[in_, ident], writes=[out])

        def act(out, in_, func, bias=None, scale=1.0):
            rd = [in_] + ([bias] if bias is not None and not isinstance(bias, float) else []) + \
                 ([scale] if not isinstance(scale, float) else [])
            if bias is None:
                P.add("act", lambda e: e.activation(out=out, in_=in_, func=func, scale=scale), reads=rd, writes=[out])
            else:
                P.add("act", lambda e: e.activation(out=out, in_=in_, func=func, bias=bias, scale=scale),
                      reads=rd, writes=[out])

        def ts(eng, out, in0, s1, s2, op0, op1=None):
            rd = [in0] + [s for s in (s1, s2) if s is not None and not isinstance(s, float)]
            if op1 is None:
                P.add(eng, lambda e: e.tensor_scalar(out=out, in0=in0, scalar1=s1, scalar2=None, op0=op0),
                      reads=rd, writes=[out])
            else:
                P.add(eng, lambda e: e.tensor_scalar(out=out, in0=in0, scalar1=s1, scalar2=s2, op0=op0, op1=op1),
                      reads=rd, writes=[out])

        def stt(eng, out, in0, scalar, in1, op0, op1):
            rd = [in0, in1] + ([scalar] if not isinstance(scalar, float) else [])
            P.add(eng, lambda e: e.scalar_tensor_tensor(out=out, in0=in0, scalar=scalar, in1=in1, op0=op0, op1=op1),
                  reads=rd, writes=[out])

        def tt(eng, out, in0, in1, op):
            P.add(eng, lambda e: e.tensor_tensor(out=out, in0=in0, in1=in1, op=op), reads=[in0, in1], writes=[out])

        def cp(eng, out, in_):
            if eng == "act":
                act(out, in_, AF.Copy)
            else:
                P.add(eng, lambda e: e.tensor_copy(out=out, in_=in_), reads=[in_], writes=[out])

        def recip(out, in_):
            P.add("dve", lambda e: e.reciprocal(out=out, in_=in_), reads=[in_], writes=[out])

        def rsqrt_eps(out, in_):
            act(out, in_, AF.Ln, bias=epsc[:, 0:1])
            act(out, out, AF.Exp, scale=-0.5)

        def allgather(out, in_):
            P.add("pool", lambda e: e.collective_compute("AllGather", ALU.bypass, replica_groups=RG,
                                                         ins=[in_.opt()], outs=[out.opt()]),
                  reads=[in_], writes=[out], dma=True, inc=1)

        nc_ctx = nc.allow_non_contiguous_dma(reason="tiny parameter layouts")
        es.enter_context(nc_ctx)

        P.add("dve", lambda e: e.memset(epsc[:], EPS), writes=[epsc[:]])
        dma("sp", identf[:], identf_d)

        def late_consts():
            dma("sp", mats[:], mats_d)
            dma("sp", rope[:], rope_d)

        def attn_consts():
            dma("sp", wbd[:], wbd_d)
            dma("sp", masks[:], masks_d)
            dma("sp", onesz[:], onesz_d)
            dma("sp", hmask[:], hmask_d)

        dma("pool", g1[:], norm1.rearrange("l (c p) -> p l c", p=128))
        for w in range(2):
            dma("pool", gq[64 * w:64 * w + 64, :], q_norm.rearrange("l d -> d l"))
            dma("pool", gk[64 * w:64 * w + 64, :], k_norm.rearrange("l d -> d l"))

        def late_params():
            dma("pool", g2[:], norm2.rearrange("l (c p) -> p l c", p=128))
            dma("pool", gF[:], g_f.rearrange("l (c p) -> p l c", p=128))
            dma("pool", bF[:], b_f.rearrange("l (c p) -> p l c", p=128))
            for w in range(2):
                for l_ in range(DEPTH):
                    dma("pool", gA[64 * w:64 * w + 64, l_, :],
                        g_a[l_:l_ + 1, 256 * w:256 * w + 256].rearrange("o (i d) -> d (o i)", d=64))
                    dma("pool", esink[64 * w:64 * w + 64, l_, :], sink[l_:l_ + 1, 4 * w:4 * w + 4].partition_broadcast(64))
            dma("pool", cb[:], conv_b.rearrange("l (c p) -> p l c", p=128))
            for l_ in range(DEPTH):
                for j_ in range(3):
                    dma("pool", cw[:, l_, j_, :], conv_w[l_, j_:j_ + 1, :].rearrange("o (c p) -> p (o c)", p=128))

        def load_win(l):
            win = c16(0, [8, 1280])
            wl = w_in[l]
            dma("pool", win[:, :, 1024:1280], wl[:, 1024:1280].rearrange("(c p) n -> p c n", p=128))
            for w in range(2):
                for kc in range(NKC):
                    dma("pool", win[:, kc, 512:1024].rearrange("p (j w d) -> p j w d", j=4, w=2, d=64)[:, :, w, :],
                        wl[kc * 128:(kc + 1) * 128, 512 + 256 * w:512 + 256 * w + 256].rearrange(
                            "p (j d) -> p j d", d=64))
            dma("pool", win[:, :, 0:512], wl[:, 0:512].rearrange("(c p) n -> p c n", p=128))

        xin = [c32(1024 * i, [1024]) for i in range(4)]
        xin_late = [c32(3584, [1024]), c32(4608, [1024])]
        win0 = c16(0, [8, 1280])

        def x_tile(t):
            xi = xin[t % 4] if t < 4 else xin_late[t % 2]
            xsrc = x[t * 128:(t + 1) * 128, :]
            if t < 4:
                dma("sp", xi, xsrc)
            else:
                P.add("sp", (lambda o_, i_: (lambda e: e.dma_start(out=o_, in_=i_)))(xi, xsrc),
                      reads=[xsrc, win0], writes=[xi], dma=True)
            for hf in range(2):
                bk = banks[(2 * t + hf) % 4] if t < 4 else banks[(2, 3, 6, 7)[(2 * t + hf) % 4]]
                for j in range(4):
                    kc = hf * 4 + j
                    tr(bk[:, j * 128:(j + 1) * 128], xi[:, kc * 128:(kc + 1) * 128], identf[:])
                cp("act" if hf == 0 else "dve", xT[:, hf * 4:hf * 4 + 4, t * 128:(t + 1) * 128],
                   bk[:].rearrange("p (a b) -> p a b", a=4, b=128))

        for t in range(4):
            x_tile(t)
        late_consts()
        load_win(0)

        def rmsnorm_T(l, g, gain, dst_fn, sq_bufs, rstd_buf, bank):
            tok = slice(g * 512, (g + 1) * 512)
            for kc in range(NKC):
                sq = sq_bufs[kc % 2]
                act(sq, xT[:, kc, tok], AF.Square)
                mm(bank[:], o1024, sq, kc == 0, kc == NKC - 1)
            rsqrt_eps(rstd_buf, bank[:])
            for kc in range(NKC):
                stt("dve", dst_fn(kc), xT[:, kc, tok], gain[:, l, kc:kc + 1], rstd_buf, ALU.mult, ALU.mult)

        xo = [c32(0, [1024]), c32(1024, [1024])]
        wb_done = []

        def writeback(t):
            xw = xo[t % 2]
            for hf in range(2):
                bk = banks[(2 * t + hf) % 4]
                for j in range(4):
                    kc = hf * 4 + j
                    tr(bk[:, j * 128:(j + 1) * 128], xT[:, kc, t * 128:(t + 1) * 128], identf[:])
                cp("act" if hf == 0 else "dve", xw[:, hf * 512:(hf + 1) * 512], bk[:])
            dma("sp", y[t * 128:(t + 1) * 128, :], xw)
            wb_done.append(t)

        for l in range(depth):
            win = c16(0, [8, 1280])
            qT = c16(10240, [4, T])
            kT = c16(18432, [2304])
            vz = c16(20736, [18, 2, 128])
            hTb = [c16(25344, [8, 512]), c16(34560, [8, 512])]
            sqb = [c16(29440, [512]), c16(29952, [512])]
            qnb = [c16(30464, [512]), c16(30976, [512])]
            usb = [c16(31488, [512]), c16(32000, [512])]
            vTb = c16(32512, [512])
            PT = [c16(33024 + 512 * i, [512]) for i in range(3)]
            mixA = c16(39424, [4, T])
            rstd = c32(0, [512])
            r2 = [c32(512, [512]), c32(1024, [512])]
            t1 = [c32(1536, [512]), c32(2048, [512])]
            t2 = [c32(2560, [512]), c32(3072, [512])]
            rden = [c32(3584, [512]), c32(4096, [512]), c32(1024, [512])]
            yA = [c32(4608, [512]), c32(5120, [512]), c32(512, [512])]
            rA = [c32(5632, [128]), c32(5760, [128])]

            if l > 0:
                load_win(l)
            P.add("pool", lambda e: e.memset(vz, 0.0), writes=[vz])

            tpb = banks[7][:].bitcast(BF16)

            def normA(g):
                tok_ = slice(g * 512, (g + 1) * 512)
                for kc in range(NKC):
                    sq = sqb[kc % 2]
                    act(sq, xT[:, kc, tok_], AF.Square)
                    mm(banks[0][:], o1024, sq, kc == 0, kc == NKC - 1)
                rsqrt_eps(rstd, banks[0][:])

            def normB(g):
                tok_ = slice(g * 512, (g + 1) * 512)
                for kc in range(NKC):
                    stt("dve", hTb[g % 2][:, kc, :], xT[:, kc, tok_], g1[:, l, kc:kc + 1], rstd, ALU.mult, ALU.mult)

            normA(0)
            normB(0)
            for g in range(4):
                tok = slice(g * 512, (g + 1) * 512)
                hT = hTb[g % 2]

                def proj(bank, c0):
                    for kc in range(NKC):
                        mm(bank[:], win[:, kc, c0:c0 + 128], hT[:, kc, :], kc == 0, kc == NKC - 1)

                def qk_p1(ps, gain, i):
                    sq = sqb[i % 2]
                    act(sq, ps, AF.Square)
                    mm(banks[2][:], bones, sq, True, True)
                    rsqrt_eps(r2[i % 2], banks[2][:])
                    stt("dve", qnb[i % 2], ps, gain[:, l:l + 1], r2[i % 2], ALU.mult, ALU.mult)

                def qk_p2(dst, i):
                    mm(banks[3][:], Rm, qnb[i % 2], True, True)
                    tt("dve", t1[i % 2], qnb[i % 2], rope[:, 0, tok], ALU.mult)
                    tt("dve", t2[i % 2], banks[3][:], rope[:, 1, tok], ALU.mult)
                    tt("pool", dst, t1[i % 2], t2[i % 2], ALU.add)

                def v_post():
                    cp("act", vTb, banks[4][:])
                    for j in range(4):
                        tr(tpb[:, j * 128:(j + 1) * 128], vTb[:, j * 128:(j + 1) * 128], identb)
                    for w in range(2):
                        cp("act", vz[:, 1 + 4 * g:5 + 4 * g, w, 64 * w:64 * w + 64],
                           tpb[:, 0:512].rearrange("p (j c) -> p j c", j=4, c=128)[:, :, 64 * w:64 * w + 64])

                def u_tile(tq, bk):
                    for kc in range(NKC):
                        mm(bk[:], hT[:, kc, tq * 128:(tq + 1) * 128], win[:, kc, 0:512], kc == 0, kc == NKC - 1)
                    cp("act", usb[tq % 2], bk[:])
                    r0 = (4 * g + tq) * 128
                    dma("sp", ub[r0:r0 + 128, :], usb[tq % 2])

                kdst = kT[:, 128 + g * 512:128 + (g + 1) * 512]
                proj(banks[1], 1024)
                proj(banks[4], 1152)
                proj(banks[5], 512)
                if l == 0 and g + 1 < 4:
                    for t_ in range(4 * (g + 1), 4 * (g + 2)):
                        x_tile(t_)
                if l == 0 and g == 2:
                    attn_consts()
                    late_params()
                if g + 1 < 4:
                    normA(g + 1)
                qk_p1(banks[1][:], gk, 0)
                proj(banks[6], 640)
                qk_p2(kdst, 0)
                v_post()
                qk_p1(banks[5][:], gq, 1)
                u_tile(0, banks[1])
                qk_p2(qT[:, 0, tok], 1)
                if g + 1 < 4:
                    normB(g + 1)
                proj(banks[5], 768)
                qk_p1(banks[6][:], gq, 2)
                u_tile(1, banks[4])
                qk_p2(qT[:, 1, tok], 2)
                proj(banks[6], 896)
                qk_p1(banks[5][:], gq, 3)
                u_tile(2, banks[1])
                qk_p2(qT[:, 2, tok], 3)
                qk_p1(banks[6][:], gq, 4)
                u_tile(3, banks[4])
                qk_p2(qT[:, 3, tok], 4)
                if g == 0 or g == 3:
                    fl = 0 if g == 0 else 1
                    blk = 1 if g == 0 else 16
                    dma("sp", kvb[fl * 128:(fl + 1) * 128, 0:128], kT[:, blk * 128:(blk + 1) * 128])
                    for w in range(2):
                        dma("sp", kvb[fl * 128:(fl + 1) * 128, 128 + 64 * w:192 + 64 * w],
                            vz[:, blk, w, 64 * w:64 * w + 64])
            allgather(kvg, kvb)
            allgather(ug, ub)
            for (blk, r0) in ((0, 128), (17, 256)):
                dma("sp", kT[:, blk * 128:(blk + 1) * 128], kvg[r0:r0 + 128, 0:128])
                for w in range(2):
                    dma("sp", vz[:, blk, w, 64 * w:64 * w + 64], kvg[r0:r0 + 128, 128 + 64 * w:192 + 64 * w])

            if l == 0:
                act(esink[:], esink[:], AF.Exp)
            norder = list(range(1, 15)) + [0, 15]
            npos = {n: i for i, n in enumerate(norder)}
            PTb = [c16(512 * i, [512]) for i in range(12)]
            acc_num, acc_den = banks[0], banks[1]
            st_banks = [banks[2], banks[3], banks[4], banks[5], banks[6]]
            msb = banks[7]
            gctr = [0]

            def tile_scores(n):
                par = npos[n] % 2
                for it in range(6):
                    mi, w = it // 2, it % 2
                    b_ = n + mi
                    st = st_banks[gctr[0] % 5]
                    gctr[0] += 1
                    pt = PTb[par * 6 + it]
                    mm(st[:], kT[64 * w:64 * w + 64, b_ * 128:(b_ + 1) * 128],
                       qT[64 * w:64 * w + 64, :, n * 128:(n + 1) * 128], True, True)
                    act(pt, st[:], AF.Exp, scale=0.125)
                    if mi != 1:
                        if mi == 0:
                            mk = masks[:, 2, :] if n == 0 else masks[:, 0, :]
                        else:
                            mk = masks[:, 3, :] if n == 15 else masks[:, 1, :]
                        tt("pool" if mi == 0 else "dve", pt, pt, mk, ALU.mult)

            def tile_pv(n):
                par = npos[n] % 2
                for it in range(6):
                    mi, w = it // 2, it % 2
                    b_ = n + mi
                    mm(acc_num[:], vz[:, b_, w, :], PTb[par * 6 + it], it == 0, it == 5)
                for it in range(6):
                    mi, w = it // 2, it % 2
                    mm(acc_den[:], onesz[:, w, :], PTb[par * 6 + it], it == 0, it == 5)

            esx = c32(5888, [512])
            for i in range(4):
                act(esx[:, i * 128:(i + 1) * 128], rope[:, 0, 0:128], AF.Identity, bias=esink[:, l, i:i + 1], scale=0.0)

            def tile_fin_a1(n):
                par = npos[n] % 2
                p3 = npos[n] % 3
                tt("dve", rden[p3], acc_den[:], esx, ALU.add)
                cp("dve", yA[p3], acc_num[:])

            def tile_fin_a2(n):
                par = npos[n] % 2
                rd, ya, sq = rden[npos[n] % 3], yA[npos[n] % 3], sqb[par]
                act(rd, rd, AF.Ln)
                act(rd, rd, AF.Exp, scale=-1.0)
                tt("dve", ya, ya, rd, ALU.mult)
                tt("pool", sq, ya, ya, ALU.mult)

            def tile_fin_b1(n):
                par = npos[n] % 2
                sq = sqb[par]
                for i in range(4):
                    mm(msb[:, 0:128], o512, sq[:, i * 128:(i + 1) * 128], i == 0, i == 3)

            def tile_fin_b2(n):
                par = npos[n] % 2
                ya = yA[npos[n] % 3]
                rsqrt_eps(rA[par], msb[:, 0:128])
                for i in range(4):
                    stt("dve", mixA[:, i, :].rearrange("p (ka kpl) -> p kpl ka", kpl=64)[:, 4 * n:4 * n + 4, :],
                        ya[:, i * 128:(i + 1) * 128].rearrange("p (a b) -> p a b", a=4, b=32), gA[:, l, i:i + 1],
                        rA[par].rearrange("p (a b) -> p a b", a=4, b=32), ALU.mult, ALU.mult)

            tile_scores(norder[0])
            for t in range(16):
                n = norder[t]
                if t >= 2:
                    tile_fin_b1(norder[t - 2])
                if t + 1 < 16:
                    tile_scores(norder[t + 1])
                if t >= 1:
                    tile_fin_a2(norder[t - 1])
                tile_pv(n)
                tile_fin_a1(n)
                if t >= 2:
                    tile_fin_b2(norder[t - 2])
            tile_fin_a2(norder[15])
            for t_ in (14, 15):
                tile_fin_b1(norder[t_])
                tile_fin_b2(norder[t_])

            Ua = [c16(2048 * i, [2048]) for i in range(2)]
            Ysb = [c16(25344, [2048]), c16(27392, [2048]), c16(34560, [2048]), c16(36608, [2048])]
            ccs = c16(16400, [2, 4, 512])
            wf = c16(20496, [4, 512])
            Yl = [c16(22544, [4, 2, 512]), c16(26640, [4, 2, 512])]
            Tl = [c16(30736, [4, 2, 128]), c16(31760, [4, 2, 128])]
            PQ = c16(32784, [2, 4, KG])
            Zt = c16(34832, [4, KG])
            mixF = c16(35856, [4, KG])
            sqf = [c16(36880, [KG]), c16(37136, [KG])]
            wo = c16(4096, [8, 1024])
            yF = c32(0, [4, KG])
            rF = c32(1024, [KG])

            dma("sp", ccs, ccs_d)
            dma("pool", wf, w_f[l].rearrange("(c p) n -> p c n", p=128))

            dma("pool", wo[:, 0:4, :], w_o[l, 0:512, :].rearrange("(c p) n -> p c n", p=128))
            for w in range(2):
                dma("pool", wo[64 * w:64 * w + 64, 4:8, :],
                    w_o[l, 512 + 256 * w:512 + 256 * w + 256, :].rearrange("(i d) n -> d i n", d=64))
            ugv = ug.rearrange("(a p) c -> a (p c)", p=128)
            def s1_load(sc):
                for j in range(4):
                    dma("sp", Ua[sc % 2][32 * j:32 * j + 32, :],
                        ugv[:, 16384 * j + 2048 * sc:16384 * j + 2048 * (sc + 1)])

            s1_load(0)
            for sc in range(8):
                if sc + 1 < 8:
                    s1_load(sc + 1)
                ysb = Ysb[sc % 4]
                for m in range(4):
                    bk = banks[(sc % 2) * 4 + m]
                    mm(bk[:], wbd[:, :], Ua[sc % 2][:, m * 512:(m + 1) * 512], True, True)
                    cp("act" if m % 2 == 0 else "dve", ysb[:, m * 512:(m + 1) * 512], bk[:])
                for j in range(4):
                    c_ = 16384 * j + 2048 * sc
                    dma("act", ys[:, c_:c_ + 2048], ysb[32 * j:32 * j + 32, :])

            mcs = c16(12288, [2, 4, 512])
            for ri in range(2):
                for cch in range(4):
                    bk = banks[4 + (ri * 4 + cch) % 4]
                    for c2 in range(4):
                        mm(bk[:], ccs[:, ri, c2, cch * 128:(cch + 1) * 128], wf[:, c2, :], c2 == 0, c2 == 3)
                    cp("act" if cch % 2 == 0 else "dve", mcs[:, ri, cch, :], bk[:])

            def stage2(g, ccs_=None):
                yl = Yl[g % 2]
                tl = Tl[g % 2]
                if ccs_ is None or 0 in ccs_:
                    if g < 4:
                        for ri in range(2):
                            r0_ = ri * 16 + 4 * g
                            dma("sp", yl[:, :, ri, :], ys[r0_:r0_ + 4, :].rearrange("j (p c) -> p j c", p=128))
                    else:
                        for j in range(4):
                            ka_ = 4 * g + j
                            rrow = ka_ if ka_ <= 16 else 32 - ka_
                            irow = 48 - ka_ if ka_ >= 17 else 17
                            dma("sp", yl[:, j, 0, :], ys[rrow:rrow + 1, :].rearrange("o (p c) -> p (o c)", p=128))
                            dma("sp", yl[:, j, 1, :], ys[irow:irow + 1, :].rearrange("o (p c) -> p (o c)", p=128))
                    dma("sp", tl, t2_d[g])
                for cc in (range(4) if ccs_ is None else ccs_):
                    for j in range(4):
                        o_ = banks[cc][:, j * 128:(j + 1) * 128]
                        mm(o_, yl[:, j, 0, cc * 128:(cc + 1) * 128], tl[:, j, 0, :], True, False)
                        mm(o_, yl[:, j, 1, cc * 128:(cc + 1) * 128], tl[:, j, 1, :], False, True)

            def evac_group():
                for cc in range(4):
                    bv = banks[cc][:].rearrange("p (j x) -> p j x", j=4, x=128)
                    e_ = "act" if cc % 2 == 0 else "dve"
                    cp(e_, PQ[:, 0, cc, :].rearrange("p (j x) -> p j x", j=4, x=64), bv[:, :, 0:64])
                    cp(e_, PQ[:, 1, cc, :].rearrange("p (j x) -> p j x", j=4, x=64), bv[:, :, 64:128])

            mixFb = [mixF, Zt]

            def wo_part(g, gi_, dcs):
                mf = mixFb[gi_ % 2]
                for dc in dcs:
                    ob = banks[6 + (dc % 2)][:, KG:2 * KG]
                    for i in range(4):
                        mm(ob, wo[:, i, dc * 128:(dc + 1) * 128], mf[:, i, :], i == 0, False)
                    for i in range(4):
                        mm(ob, wo[:, 4 + i, dc * 128:(dc + 1) * 128], mixA[:, i, g * KG:(g + 1) * KG], False, i == 3)
                    xv = xT[:, dc, :].rearrange("p (kpl ka) -> p ka kpl", ka=32)[:, 4 * g:4 * g + 4, :]
                    tt("dve", xv, ob.rearrange("p (j x) -> p j x", j=4, x=64), xv, ALU.add)

            def post_group(g, gi_, nxt=None, prev=None):
                def fill(ccs2):
                    if nxt is not None:
                        stage2(nxt, ccs2)
                if prev is not None:
                    wo_part(prev, gi_ - 1, range(0, 4))
                for c3 in range(4):
                    yb = banks[4 + (c3 % 2)][:, 0:KG]
                    for cc in range(4):
                        mm(yb, mcs[:, 0, cc, c3 * 128:(c3 + 1) * 128], PQ[:, 0, cc, :], cc == 0, False)
                        mm(yb, mcs[:, 1, cc, c3 * 128:(c3 + 1) * 128], PQ[:, 1, cc, :], False, cc == 3)
                    act(yF[:, c3, :], yb, AF.Identity, bias=bF[:, l, c3:c3 + 1])
                fill([0, 1])
                for c3 in range(4):
                    act(sqf[c3 % 2], yF[:, c3, :], AF.Square)
                    mm(banks[5][:, KG:2 * KG], o512, sqf[c3 % 2], c3 == 0, c3 == 3)
                if prev is not None:
                    wo_part(prev, gi_ - 1, range(4, 8))
                rsqrt_eps(rF, banks[5][:, KG:2 * KG])
                mf = mixFb[gi_ % 2]
                for c3 in range(4):
                    stt("dve", mf[:, c3, :], yF[:, c3, :], gF[:, l, c3:c3 + 1], rF, ALU.mult, ALU.mult)
                fill([2, 3])

            sqh = c16(37392, [8, 2])
            hh = c16(37408, [8, 2])
            hstage = c16(37440, [8, 2])
            rh = c32(1280, [2])

            def early_halo():
                for kc in range(NKC):
                    act(sqh[:, kc, :], xT[:, kc, 0:2048:2047], AF.Square)
                    mm(banks[5][:, 0:2], o1024, sqh[:, kc, :], kc == 0, kc == NKC - 1)
                rsqrt_eps(rh, banks[5][:, 0:2])
                for kc in range(NKC):
                    stt("dve", hh[:, kc, :], xT[:, kc, 0:2048:2047], g2[:, l, kc:kc + 1], rh, ALU.mult, ALU.mult)
                dma("pool", hb[0:1, :].rearrange("o (c p) -> p (o c)", p=128), hh[:, :, 0])
                dma("pool", hb[1:2, :].rearrange("o (c p) -> p (o c)", p=128), hh[:, :, 1])
                allgather(hg, hb)
                dma("pool", hstage[:, :, 0], hg[1:2, :].rearrange("o (c p) -> p (o c)", p=128))
                dma("pool", hstage[:, :, 1], hg[2:3, :].rearrange("o (c p) -> p (o c)", p=128))

            pairs = [(grp, fh, jp) for grp in range(2) for fh in range(2) for jp in range(11)]
            wup = [c16(16400 + 2048 * i, [8, 256]) for i in range(3)]

            def load_wup(q):
                grp, fh, jp = pairs[q]
                fc = fh * 11 + jp
                wu = wup[q % 3]
                for wh in range(2):
                    col = wh * DFF + fc * 128
                    dma("pool", wu[:, :, wh * 128:(wh + 1) * 128],
                        w_up[l, :, col:col + 128].rearrange("(c p) n -> p c n", p=128))

            gorder = [7, 0, 1, 2, 3, 4, 5, 6]
            stage2(gorder[0])
            for gi, kg in enumerate(gorder):
                evac_group()
                post_group(kg, gi, gorder[gi + 1] if gi + 1 < NKG else None, gorder[gi - 1] if gi >= 1 else None)
                if gi == 2:
                    early_halo()
                    load_wup(0)
                    load_wup(1)
            wo_part(gorder[-1], NKG - 1, range(0, 8))

            hT2 = c16(0, [8, 2050])
            actT = c16(22544, [11, 1024])
            wdn = c16(33808, [11, 1024])
            wup = [c16(16400 + 2048 * i, [8, 256]) for i in range(3)]
            sq2 = [c16(45072, [512]), c16(45584, [512])]
            sg = c16(46096, [1024])
            xs = [[c32(0, [1026]), c32(1026, [1026])], [c32(2052, [1026]), c32(3078, [1026])]]
            cv = [c32(4104, [1024]), c32(5128, [1024])]
            rstd2 = c32(6152, [504]) if False else None

            for g in range(4):
                rmsnorm_T(l, g, g2, lambda kc: hT2[:, kc, 1 + g * 512:1 + (g + 1) * 512], sq2, cv[0][:, 0:512],
                          banks[g % 2])
            ts("dve", hT2[:, :, 0], hstage[:, :, 0], hmask[:, 0:1], None, ALU.mult)
            ts("dve", hT2[:, :, 2049], hstage[:, :, 1], hmask[:, 1:2], None, ALU.mult)

            for q, (grp, fh, jp) in enumerate(pairs):
                c0 = 1 + 1024 * grp
                if jp == 0:
                    dma("pool", wdn, w_down[l, fh * 1408:(fh + 1) * 1408, :].rearrange("(j p) n -> p j n", p=128))
                if q + 2 < len(pairs):
                    load_wup(q + 2)
                fc = fh * 11 + jp
                wu = wup[q % 3]
                for wh in range(2):
                    ccol = wh * 22 + fc
                    xb = xs[q % 2][wh]
                    ub_ = [banks[2 * wh], banks[2 * wh + 1]]
                    hbk = banks[4 + wh]
                    for hf in range(2):
                        for kc in range(NKC):
                            mm(ub_[hf][:], wu[:, kc, wh * 128:(wh + 1) * 128],
                               hT2[:, kc, c0 + hf * 512:c0 + (hf + 1) * 512], kc == 0, kc == NKC - 1)
                    for kc in range(NKC):
                        mm(hbk[:, 0:2], wu[:, kc, wh * 128:(wh + 1) * 128],
                           hT2[:, kc, c0 - 1:c0 + 1025:1025], kc == 0, kc == NKC - 1)
                    for hf in range(2):
                        cp("act", xb[:, 1 + hf * 512:513 + hf * 512], ub_[hf][:])
                        act(cv[wh][:, hf * 512:(hf + 1) * 512], ub_[hf][:], AF.Identity,
                            bias=cb[:, l, ccol:ccol + 1], scale=cw[:, l, 1, ccol:ccol + 1])
                    cp("act", xb[:, 0:1026:1025], hbk[:, 0:2])
                    stt("dve", cv[wh], xb[:, 0:1024], cw[:, l, 0, ccol:ccol + 1], cv[wh], ALU.mult, ALU.add)
                    stt("dve", cv[wh], xb[:, 2:1026], cw[:, l, 2, ccol:ccol + 1], cv[wh], ALU.mult, ALU.add)
                act(sg, cv[0], AF.Silu)
                tt("pool", actT[:, jp, :], sg, cv[1], ALU.mult)
                if jp == 10:
                    last = (l == depth - 1 and grp == 1 and fh == 1)
                    if not last:
                        for dc in range(NKC):
                            for hf in range(2):
                                db = banks[6 + ((dc * 2 + hf) % 2)]
                                for j2 in range(11):
                                    mm(db[:], wdn[:, j2, dc * 128:(dc + 1) * 128],
                                       actT[:, j2, hf * 512:(hf + 1) * 512], j2 == 0, j2 == 10)
                                tk = slice(grp * 1024 + hf * 512, grp * 1024 + (hf + 1) * 512)
                                tt("dve", xT[:, dc, tk], db[:], xT[:, dc, tk], ALU.add)
                    else:
                        pend = list(range(8))
                        for hf in range(2):
                            for dc in range(NKC):
                                db = banks[6 + (dc % 2)]
                                for j2 in range(11):
                                    mm(db[:], wdn[:, j2, dc * 128:(dc + 1) * 128],
                                       actT[:, j2, hf * 512:(hf + 1) * 512], j2 == 0, j2 == 10)
                                tk = slice(grp * 1024 + hf * 512, grp * 1024 + (hf + 1) * 512)
                                tt("dve", xT[:, dc, tk], db[:], xT[:, dc, tk], ALU.add)
                                if pend and (hf == 1 or dc % 2 == 1):
                                    writeback(pend.pop(0))
                            if hf == 0:
                                pend += [8, 9, 10, 11]

        for t in range(16):
            if t not in wb_done:
                writeback(t)

        P.emit(sem_c)
    return nc


def _consts(h):
    bf = ml_dtypes.bfloat16
    a = np.arange(32, dtype=np.float64)
    ang1 = 2.0 * np.pi * (a[:, None] * a[None, :]) / 32.0
    w32 = np.concatenate([np.cos(ang1), np.sin(ang1)], axis=1)
    w32r = np.concatenate([np.cos(ang1)[:, 0:17], np.sin(ang1)[:, 1:16]], axis=1)
    wbd = np.zeros((4, 32, 4, 32), np.float64)
    for j in range(4):
        wbd[j, :, j, :] = w32r
    wbd = wbd.reshape(128, 128).astype(bf)
    p = np.arange(128, dtype=np.int64)
    ka = np.arange(32, dtype=np.int64)
    kpl = np.arange(64, dtype=np.int64)
    kk = ka[:, None] + 32 * (kpl[None, :] + 64 * h)
    th = 2.0 * np.pi * ((p[:, None, None] * kk[None, :, :]) % SEQ).astype(np.float64) / SEQ
    T1 = np.concatenate([np.cos(th), np.sin(th)], axis=2)
    T2 = np.concatenate([-np.sin(th), np.cos(th)], axis=2)
    sgn = np.where(ka > 16, -1.0, 1.0) * np.where((ka == 0) | (ka == 16), 0.0, 1.0)
    T2 = T2 * sgn[None, :, None]
    t2 = np.stack([T1, T2], axis=2)
    t2 = t2.reshape(128, NKG, 4, 2, 128).transpose(1, 0, 2, 3, 4)
    t2 = np.ascontiguousarray(t2).astype(bf)
    c = np.arange(512, dtype=np.int64)
    a2 = 2.0 * np.pi * ((c[:, None] * c[None, :]) % 512).astype(np.float64) / 512
    scale = 1.0 / np.sqrt(float(SEQ) * 512.0)
    cc = np.stack([np.cos(a2) * scale, -np.sin(a2) * scale], axis=0)
    ccs = cc.reshape(2, 4, 128, 512).transpose(2, 0, 1, 3)
    ccs = np.ascontiguousarray(ccs).astype(bf)
    inv = 1.0 / (10000.0 ** (np.arange(0, 64, 2, dtype=np.float32) / 64.0))
    pos = (2048 * h + np.arange(T)).astype(np.float32)
    angr = pos[None, :] * inv[:, None].astype(np.float32)
    idx = (np.arange(128) % 64) % 32
    rope = np.stack([np.cos(angr)[idx], np.sin(angr)[idx]], axis=1)
    rope = np.ascontiguousarray(rope).astype(bf)
    mats = np.zeros((128, 5, 128), np.float32)
    for m in range(128):
        d = m % 64
        if d < 32:
            mats[m + 32, 0, m] = -1.0
        else:
            mats[m - 32, 0, m] = 1.0
    mats[:64, 1, :64] = 1.0 / 64
    mats[64:, 1, 64:] = 1.0 / 64
    mats[:, 2, :] = 1.0 / 1024
    mats[:, 3, :] = 1.0 / 512
    mats[:, 4, :] = np.eye(128)
    jj = np.arange(128)[:, None]
    qi = np.arange(128)[None, :]
    prev = (jj >= qi).astype(np.float32)
    nxt = (jj <= qi).astype(np.float32)
    zero = np.zeros_like(prev)
    mk = np.stack([prev, nxt, prev if h == 1 else zero, nxt if h == 0 else zero], axis=1)
    masks = np.tile(mk[:, :, None, :], (1, 1, 4, 1)).reshape(128, 4, 512)
    onesz = np.zeros((128, 2, 128), np.float32)
    onesz[:, 0, :64] = 1.0
    onesz[:, 1, 64:] = 1.0
    hmask = np.zeros((128, 2), np.float32)
    hmask[:, 0] = 1.0 if h == 1 else 0.0
    hmask[:, 1] = 1.0 if h == 0 else 0.0
    return {
        "c_wbd": wbd, "c_t2": t2, "c_ccs": ccs, "c_rope": rope, "c_mats": mats.astype(bf),
        "c_identf": np.eye(128, dtype=np.float32), "c_masks": masks.astype(bf),
        "c_onesz": onesz.astype(bf), "c_hmask": hmask,
    }


_NC_CACHE = {}


def kernel(x, norm1, w_in, w_fourier, b_fourier, q_norm, k_norm, sink, g_fourier_out, g_attn_out, w_o,
           norm2, w_up, conv_w, conv_b, w_down):
    f = lambda a: np.ascontiguousarray(np.asarray(a, dtype=np.float32))
    x = f(x)
    shared = {
        "norm1": f(norm1), "norm2": f(norm2), "w_in": f(w_in), "w_fourier": f(w_fourier),
        "b_fourier": f(b_fourier), "q_norm": f(q_norm), "k_norm": f(k_norm), "sink": f(sink),
        "g_fourier_out": f(g_fourier_out), "g_attn_out": f(g_attn_out), "w_o": f(w_o),
        "w_up": f(w_up), "conv_w": f(conv_w), "conv_b": f(conv_b), "w_down": f(w_down),
    }
    consts = [_consts(0), _consts(1)]
    if "nc" not in _NC_CACHE:
        _NC_CACHE["nc"] = build_nc()
    nc = _NC_CACHE["nc"]
    in_maps = []
    for c in range(8):
        b, h = c // 2, c % 2
        m = dict(shared)
        m.update(consts[h])
        m["x"] = np.ascontiguousarray(x[b, h * T:(h + 1) * T, :])
        in_maps.append(m)
    res = run_bass_kernel_spmd(nc, in_maps, core_ids=list(range(8)))
    out = np.empty((4, SEQ, D), np.float32)
    for c in range(8):
        b, h = c // 2, c % 2
        out[b, h * T:(h + 1) * T, :] = res.results[c]["y"]
    return out
```

```python
import numpy as np
import ml_dtypes
import concourse.bass as bass
import concourse.mybir as mybir
from concourse.bass_utils import run_bass_kernel_spmd

F32, BF16 = mybir.dt.float32, mybir.dt.bfloat16
AF = mybir.ActivationFunctionType
ALU = mybir.AluOpType

D = 1024
T = 2048
SEQ = 4096
DEPTH = 2
DFF = 2816
EPS = 1e-6
NKC = 8
KG = 256
NKG = T // KG
RG = [[0, 1], [2, 3], [4, 5], [6, 7]]


def _rect(ap, whole=False):
    name = ap.tensor.name
    dims = ap.ap
    off = ap.offset
    sp = str(ap.space)
    if "DRAM" in sp.upper() or "HBM" in sp.upper():
        shp = tuple(ap.tensor.shape)
        if len(shp) == 2:
            C = int(shp[1])
            rext = 0
            cext = 0
            for st_, c in dims:
                st_ = abs(int(st_))
                if st_ % C == 0:
                    rext += (c - 1) * (st_ // C)
                else:
                    cext += (c - 1) * st_
            r0, c0 = off // C, off % C
            if c0 + cext < C:
                return (name, r0, r0 + rext + 1, c0, c0 + cext + 1)
        ext = sum((c - 1) * abs(s) for s, c in dims) + 1
        return (name, 0, 1 << 30, off, off + ext)
    if "PSUM" in sp.upper():
        return (name, 0, 128, 0, 1 << 30)
    pstep, pcnt = dims[0]
    if pstep == 0:
        return (name, 0, 128, 0, 1 << 30)
    p0 = off // pstep
    f0 = off % pstep
    ext = sum((c - 1) * abs(s) for s, c in dims[1:]) + 1
    return (name, p0, p0 + pcnt, f0, f0 + ext)


class Prog:
    ENG = ("pe", "act", "dve", "pool", "sp")

    def __init__(self, nc):
        self.nc = nc
        self.ops = []
        self.track = {}

    def add(self, eng, fn, reads=(), writes=(), dma=False, inc=16):
        oid = len(self.ops)
        deps = set()
        rr = [_rect(a) for a in reads]
        wr = [_rect(a) for a in writes]
        wr = wr + [r for r in rr if r[0].startswith("bank")]
        rr = [r for r in rr if not r[0].startswith("bank")]
        for (name, p0, p1, f0, f1) in rr:
            for rec in self.track.get(name, ()):
                if rec[5] and rec[0] < p1 and p0 < rec[1] and rec[2] < f1 and f0 < rec[3]:
                    deps.add(rec[4])
        for (name, p0, p1, f0, f1) in wr:
            lst = self.track.get(name, [])
            keep = []
            for rec in lst:
                if rec[0] < p1 and p0 < rec[1] and rec[2] < f1 and f0 < rec[3]:
                    deps.add(rec[4])
                    if p0 <= rec[0] and rec[1] <= p1 and f0 <= rec[2] and rec[3] <= f1:
                        continue
                keep.append(rec)
            self.track[name] = keep
        for (name, p0, p1, f0, f1) in rr:
            self.track.setdefault(name, []).append((p0, p1, f0, f1, oid, False))
        for (name, p0, p1, f0, f1) in wr:
            self.track.setdefault(name, []).append((p0, p1, f0, f1, oid, True))
        deps.discard(oid)
        self.ops.append(dict(eng=eng, fn=fn, deps=deps, dma=dma, inc=inc, id=oid))
        return oid

    def emit(self, sems):
        ops = self.ops
        has_dep = [False] * len(ops)
        for o in ops:
            for d in o["deps"]:
                has_dep[d] = True
        cnt = {e: 0 for e in self.ENG}
        dcnt = {}
        dq_i = {e: 0 for e in self.ENG}
        last_on_sem = {}
        for o in ops:
            e = o["eng"]
            o["pre"] = None
            if o["dma"]:
                qn = "cc" if o["inc"] != 16 else e
                pool = sems["dma_" + qn]
                s = pool[dq_i.setdefault(qn, 0) % len(pool)]
                dq_i[qn] += 1
                key = id(s)
                o["pre"] = last_on_sem.get(key)
                v = dcnt.get(key, 0) + o["inc"]
                dcnt[key] = v
                o["done"] = (s, v)
                last_on_sem[key] = (s, v)
                o["signal"] = True
            else:
                if has_dep[o["id"]]:
                    cnt[e] += 1
                    o["signal"] = True
                else:
                    o["signal"] = False
                o["done"] = (sems[e], cnt[e])
        per_eng = {e: [] for e in self.ENG}
        waited = {e: {} for e in self.ENG}
        for o in ops:
            e = o["eng"]
            w = {}
            if o["pre"] is not None:
                s, v = o["pre"]
                w[id(s)] = (s, v)
            for d in o["deps"]:
                po = ops[d]
                if (not po["dma"]) and po["eng"] == e and e == "pe":
                    continue
                s, v = po["done"]
                if id(s) not in w or w[id(s)][1] < v:
                    w[id(s)] = (s, v)
            wl = []
            for k, (s, v) in w.items():
                if waited[e].get(k, 0) < v:
                    waited[e][k] = v
                    wl.append((s, v))
            per_eng[e].append((wl, o))
        nc = self.nc
        with nc.Block() as block:
            def run(engine, lst):
                for wl, o in lst:
                    emb = None
                    if wl:
                        emb = wl[-1]
                        wl = wl[:-1]
                    for s, v in wl:
                        engine.wait_ge(s, v)
                    ins = o["fn"](engine)
                    if emb is not None:
                        ins._wait_ge(emb[0], emb[1])
                    if o["signal"]:
                        s, v = o["done"]
                        if o["dma"]:
                            if o["inc"] == 16:
                                ins.then_inc(s, 16)
                            else:
                                ins.then_inc(s)
                        else:
                            ins.then_inc(s, 1)
                return

            @block.tensor
            def _(eng):
                run(eng, per_eng["pe"])

            @block.scalar
            def _(eng):
                run(eng, per_eng["act"])
                for s in sems.get("dma_act", []):
                    v = dcnt.get(id(s), 0)
                    if v:
                        eng.wait_ge(s, v)

            @block.vector
            def _(eng):
                run(eng, per_eng["dve"])

            @block.gpsimd
            def _(eng):
                run(eng, per_eng["pool"])
                for s in sems["dma_pool"] + sems.get("dma_cc", []):
                    v = dcnt.get(id(s), 0)
                    if v:
                        eng.wait_ge(s, v)

            @block.sync
            def _(eng):
                run(eng, per_eng["sp"])
                for s in sems["dma_sp"]:
                    v = dcnt.get(id(s), 0)
                    if v:
                        eng.wait_ge(s, v)


def build_nc(depth=DEPTH, stage="full"):
    nc = bass.Bass("TRN2", target_bir_lowering=False)
    P = Prog(nc)

    def din(name, shape, dt=F32):
        return nc.dram_tensor(name, list(shape), dt, kind="ExternalInput").ap()

    x = din("x", [T, D])
    norm1 = din("norm1", [DEPTH, D]); norm2 = din("norm2", [DEPTH, D])
    w_in = din("w_in", [DEPTH, D, 1280]); w_f = din("w_fourier", [DEPTH, 512, 512])
    b_f = din("b_fourier", [DEPTH, 512]); q_norm = din("q_norm", [DEPTH, 64]); k_norm = din("k_norm", [DEPTH, 64])
    sink = din("sink", [DEPTH, 8]); g_f = din("g_fourier_out", [DEPTH, 512]); g_a = din("g_attn_out", [DEPTH, 512])
    w_o = din("w_o", [DEPTH, D, D]); w_up = din("w_up", [DEPTH, D, 2 * DFF])
    conv_w = din("conv_w", [DEPTH, 3, 2 * DFF]); conv_b = din("conv_b", [DEPTH, 2 * DFF])
    w_down = din("w_down", [DEPTH, DFF, D])
    wbd_d = din("c_wbd", [128, 128], BF16)
    t2_d = din("c_t2", [NKG, 128, 4, 2, 128], BF16)
    ccs_d = din("c_ccs", [128, 2, 4, 512], BF16)
    rope_d = din("c_rope", [128, 2, T], BF16)
    mats_d = din("c_mats", [128, 5, 128], BF16)
    identf_d = din("c_identf", [128, 128], F32)
    masks_d = din("c_masks", [128, 4, 512], BF16)
    onesz_d = din("c_onesz", [128, 2, 128], BF16)
    hmask_d = din("c_hmask", [128, 2], F32)
    y = nc.dram_tensor("y", [T, D], F32, kind="ExternalOutput").ap()

    ub = nc.dram_tensor("ub", [T, 512], BF16).ap()
    ug = nc.dram_tensor("ug", [SEQ, 512], BF16).ap()
    kvb = nc.dram_tensor("kvb", [256, 256], BF16).ap()
    kvg = nc.dram_tensor("kvg", [512, 256], BF16).ap()
    ys = nc.dram_tensor("ys", [32, 65536], BF16).ap()
    hb = nc.dram_tensor("hb", [2, 1024], BF16).ap()
    hg = nc.dram_tensor("hg", [4, 1024], BF16).ap()

    A16 = 47616
    A32 = 6656
    import contextlib
    es = contextlib.ExitStack()
    with es:
        def sb(name, shape, dt):
            return es.enter_context(nc.sbuf_tensor(name, list(shape), dt))
        xT = sb("xT", [128, NKC, T], F32)
        a16 = sb("a16", [128, A16], BF16)
        a32 = sb("a32", [128, A32], F32)
        rope = sb("rope", [128, 2, T], BF16)
        mats = sb("mats", [128, 5, 128], BF16)
        identf = sb("identf", [128, 128], F32)
        masks = sb("masks", [128, 4, 512], BF16)
        onesz = sb("onesz", [128, 2, 128], BF16)
        hmask = sb("hmask", [128, 2], F32)
        epsc = sb("epsc", [128, 1], F32)
        wbd = sb("wbd", [128, 128], BF16)
        g1 = sb("g1", [128, DEPTH, 8], F32); g2 = sb("g2", [128, DEPTH, 8], F32)
        gq = sb("gq", [128, DEPTH], F32); gk = sb("gk", [128, DEPTH], F32)
        esink = sb("esink", [128, DEPTH, 4], F32)
        gF = sb("gF", [128, DEPTH, 4], F32); gA = sb("gA", [128, DEPTH, 4], F32)
        bF = sb("bF", [128, DEPTH, 4], F32)
        cw = sb("cw", [128, DEPTH, 3, 44], F32); cb = sb("cb", [128, DEPTH, 44], F32)
        banks = [es.enter_context(nc.psum_tensor(f"bank{i}", [128, 512], F32)) for i in range(8)]
        sem_c = {e: es.enter_context(nc.semaphore("s_" + e)) for e in ("pe", "act", "dve", "pool", "sp")}
        sem_c["dma_sp"] = [es.enter_context(nc.semaphore(f"dsp{i}")) for i in range(24)]
        sem_c["dma_pool"] = [es.enter_context(nc.semaphore(f"dpl{i}")) for i in range(24)]
        sem_c["dma_cc"] = [es.enter_context(nc.semaphore(f"dcc{i}")) for i in range(3)]
        sem_c["dma_act"] = [es.enter_context(nc.semaphore(f"dac{i}")) for i in range(8)]

        Rm, bones, o1024, o512, identb = (mats[:, i, :] for i in range(5))

        def c16(off, shape):
            n = int(np.prod(shape))
            assert off + n <= A16, (off, n)
            v = a16[:, off:off + n]
            if len(shape) == 2:
                v = v.rearrange("p (a b) -> p a b", a=shape[0], b=shape[1])
            elif len(shape) == 3:
                v = v.rearrange("p (a b c) -> p a b c", a=shape[0], b=shape[1], c=shape[2])
            return v

        def c32(off, shape):
            n = int(np.prod(shape))
            assert off + n <= A32, (off, n)
            v = a32[:, off:off + n]
            if len(shape) == 2:
                v = v.rearrange("p (a b) -> p a b", a=shape[0], b=shape[1])
            return v

        def dma(eng, out, in_, **kw):
            P.add(eng, lambda e: e.dma_start(out=out, in_=in_, **kw), reads=[in_], writes=[out], dma=True)

        def mm(out, lhsT, rhs, start, stop, skip=False):
            P.add("pe", lambda e: e.matmul(out, lhsT, rhs, start=start, stop=stop, skip_group_check=skip),
                  reads=[lhsT, rhs], writes=[out])

        def tr(out, in_, ident):
            P.add("pe", lambda e: e.transpose(out, in_, ident), reads=[in_, ident], writes=[out])

        def act(out, in_, func, bias=None, scale=1.0):
            rd = [in_] + ([bias] if bias is not None and not isinstance(bias, float) else []) + \
                 ([scale] if not isinstance(scale, float) else [])
            if bias is None:
                P.add("act", lambda e: e.activation(out=out, in_=in_, func=func, scale=scale), reads=rd, writes=[out])
            else:
                P.add("act", lambda e: e.activation(out=out, in_=in_, func=func, bias=bias, scale=scale),
                      reads=rd, writes=[out])

        def ts(eng, out, in0, s1, s2, op0, op1=None):
            rd = [in0] + [s for s in (s1, s2) if s is not None and not isinstance(s, float)]
            if op1 is None:
                P.add(eng, lambda e: e.tensor_scalar(out=out, in0=in0, scalar1=s1, scalar2=None, op0=op0),
                      reads=rd, writes=[out])
            else:
                P.add(eng, lambda e: e.tensor_scalar(out=out, in0=in0, scalar1=s1, scalar2=s2, op0=op0, op1=op1),
                      reads=rd, writes=[out])

        def stt(eng, out, in0, scalar, in1, op0, op1):
            rd = [in0, in1] + ([scalar] if not isinstance(scalar, float) else [])
            P.add(eng, lambda e: e.scalar_tensor_tensor(out=out, in0=in0, scalar=scalar, in1=in1, op0=op0, op1=op1),
                  reads=rd, writes=[out])

        def tt(eng, out, in0, in1, op):
            P.add(eng, lambda e: e.tensor_tensor(out=out, in0=in0, in1=in1, op=op), reads=[in0, in1], writes=[out])

        def cp(eng, out, in_):
            if eng == "act":
                act(out, in_, AF.Copy)
            else:
                P.add(eng, lambda e: e.tensor_copy(out=out, in_=in_), reads=[in_], writes=[out])

        def recip(out, in_):
            P.add("dve", lambda e: e.reciprocal(out=out, in_=in_), reads=[in_], writes=[out])

        def rsqrt_eps(out, in_):
            act(out, in_, AF.Ln, bias=epsc[:, 0:1])
            act(out, out, AF.Exp, scale=-0.5)

        def allgather(out, in_):
            P.add("pool", lambda e: e.collective_compute("AllGather", ALU.bypass, replica_groups=RG,
                                                         ins=[in_.opt()], outs=[out.opt()]),
                  reads=[in_], writes=[out], dma=True, inc=1)

        nc_ctx = nc.allow_non_contiguous_dma(reason="tiny parameter layouts")
        es.enter_context(nc_ctx)

        P.add("dve", lambda e: e.memset(epsc[:], EPS), writes=[epsc[:]])
        dma("sp", identf[:], identf_d)

        def late_consts():
            dma("sp", mats[:], mats_d)
            dma("sp", rope[:], rope_d)

        def attn_consts():
            dma("sp", wbd[:], wbd_d)
            dma("sp", masks[:], masks_d)
            dma("sp", onesz[:], onesz_d)
            dma("sp", hmask[:], hmask_d)

        dma("pool", g1[:], norm1.rearrange("l (c p) -> p l c", p=128))
        for w in range(2):
            dma("pool", gq[64 * w:64 * w + 64, :], q_norm.rearrange("l d -> d l"))
            dma("pool", gk[64 * w:64 * w + 64, :], k_norm.rearrange("l d -> d l"))

        def late_params():
            dma("pool", g2[:], norm2.rearrange("l (c p) -> p l c", p=128))
            dma("pool", gF[:], g_f.rearrange("l (c p) -> p l c", p=128))
            dma("pool", bF[:], b_f.rearrange("l (c p) -> p l c", p=128))
            for w in range(2):
                for l_ in range(DEPTH):
                    dma("pool", gA[64 * w:64 * w + 64, l_, :],
                        g_a[l_:l_ + 1, 256 * w:256 * w + 256].rearrange("o (i d) -> d (o i)", d=64))
                    dma("pool", esink[64 * w:64 * w + 64, l_, :], sink[l_:l_ + 1, 4 * w:4 * w + 4].partition_broadcast(64))
            dma("pool", cb[:], conv_b.rearrange("l (c p) -> p l c", p=128))
            for l_ in range(DEPTH):
                for j_ in range(3):
                    dma("pool", cw[:, l_, j_, :], conv_w[l_, j_:j_ + 1, :].rearrange("o (c p) -> p (o c)", p=128))

        def load_win(l):
            win = c16(0, [8, 1280])
            wl = w_in[l]
            dma("pool", win[:, :, 1024:1280], wl[:, 1024:1280].rearrange("(c p) n -> p c n", p=128))
            for w in range(2):
                for kc in range(NKC):
                    dma("pool", win[:, kc, 512:1024].rearrange("p (j w d) -> p j w d", j=4, w=2, d=64)[:, :, w, :],
                        wl[kc * 128:(kc + 1) * 128, 512 + 256 * w:512 + 256 * w + 256].rearrange(
                            "p (j d) -> p j d", d=64))
            dma("pool", win[:, :, 0:512], wl[:, 0:512].rearrange("(c p) n -> p c n", p=128))

        xin = [c32(1024 * i, [1024]) for i in range(4)]
        xin_late = [c32(3584, [1024]), c32(4608, [1024])]
        win0 = c16(0, [8, 1280])

        def x_tile(t):
            xi = xin[t % 4] if t < 4 else xin_late[t % 2]
            xsrc = x[t * 128:(t + 1) * 128, :]
            if t < 4:
                dma("sp", xi, xsrc)
            else:
                P.add("sp", (lambda o_, i_: (lambda e: e.dma_start(out=o_, in_=i_)))(xi, xsrc),
                      reads=[xsrc, win0], writes=[xi], dma=True)
            for hf in range(2):
                bk = banks[(2 * t + hf) % 4] if t < 4 else banks[(2, 3, 6, 7)[(2 * t + hf) % 4]]
                for j in range(4):
                    kc = hf * 4 + j
                    tr(bk[:, j * 128:(j + 1) * 128], xi[:, kc * 128:(kc + 1) * 128], identf[:])
                cp("act" if hf == 0 else "dve", xT[:, hf * 4:hf * 4 + 4, t * 128:(t + 1) * 128],
                   bk[:].rearrange("p (a b) -> p a b", a=4, b=128))

        for t in range(4):
            x_tile(t)
        late_consts()
        load_win(0)

        def rmsnorm_T(l, g, gain, dst_fn, sq_bufs, rstd_buf, bank):
            tok = slice(g * 512, (g + 1) * 512)
            for kc in range(NKC):
                sq = sq_bufs[kc % 2]
                if kc in (2, 5, 7):
                    tt("pool", sq, xT[:, kc, tok], xT[:, kc, tok], ALU.mult)
                else:
                    act(sq, xT[:, kc, tok], AF.Square)
                mm(bank[:], o1024, sq, kc == 0, kc == NKC - 1)
            rsqrt_eps(rstd_buf, bank[:])
            for kc in range(NKC):
                stt("dve", dst_fn(kc), xT[:, kc, tok], gain[:, l, kc:kc + 1], rstd_buf, ALU.mult, ALU.mult)

        xo = [c32(0, [1024]), c32(1024, [1024])]
        wb_done = []

        def writeback(t):
            xw = xo[t % 2]
            for hf in range(2):
                bk = banks[(2 * t + hf) % 4]
                for j in range(4):
                    kc = hf * 4 + j
                    tr(bk[:, j * 128:(j + 1) * 128], xT[:, kc, t * 128:(t + 1) * 128], identf[:])
                cp("act" if hf == 0 else "dve", xw[:, hf * 512:(hf + 1) * 512], bk[:])
            dma("sp", y[t * 128:(t + 1) * 128, :], xw)
            wb_done.append(t)

        for l in range(depth):
            win = c16(0, [8, 1280])
            qT = c16(10240, [4, T])
            kT = c16(18432, [2304])
            vz = c16(20736, [18, 2, 128])
            hTb = [c16(25344, [8, 512]), c16(34560, [8, 512])]
            sqb = [c16(29440, [512]), c16(29952, [512])]
            qnb = [c16(30464, [512]), c16(30976, [512])]
            usb = [c16(31488, [512]), c16(32000, [512])]
            vTb = c16(32512, [512])
            PT = [c16(33024 + 512 * i, [512]) for i in range(3)]
            mixA = c16(39424, [4, T])
            rstd = c32(0, [512])
            r2 = [c32(512, [512]), c32(1024, [512])]
            t1 = [c32(1536, [512]), c32(2048, [512])]
            t2 = [c32(2560, [512]), c32(3072, [512])]
            rden = [c32(3584, [512]), c32(4096, [512]), c32(1024, [512])]
            yA = [c32(4608, [512]), c32(5120, [512]), c32(512, [512])]
            rA = [c32(5632, [128]), c32(5760, [128])]

            if l > 0:
                load_win(l)
            P.add("pool", lambda e: e.memset(vz, 0.0), writes=[vz])

            tpb = banks[7][:].bitcast(BF16)

            def normA(g):
                tok_ = slice(g * 512, (g + 1) * 512)
                for kc in range(NKC):
                    sq = sqb[kc % 2]
                    act(sq, xT[:, kc, tok_], AF.Square)
                    mm(banks[0][:], o1024, sq, kc == 0, kc == NKC - 1)
                rsqrt_eps(rstd, banks[0][:])

            def normB(g):
                tok_ = slice(g * 512, (g + 1) * 512)
                for kc in range(NKC):
                    stt("dve", hTb[g % 2][:, kc, :], xT[:, kc, tok_], g1[:, l, kc:kc + 1], rstd, ALU.mult, ALU.mult)

            normA(0)
            normB(0)
            for g in range(4):
                tok = slice(g * 512, (g + 1) * 512)
                hT = hTb[g % 2]

                def proj(bank, c0):
                    for kc in range(NKC):
                        mm(bank[:], win[:, kc, c0:c0 + 128], hT[:, kc, :], kc == 0, kc == NKC - 1)

                def qk_p1(ps, gain, i):
                    sq = sqb[i % 2]
                    act(sq, ps, AF.Square)
                    mm(banks[2][:], bones, sq, True, True)
                    rsqrt_eps(r2[i % 2], banks[2][:])
                    stt("dve", qnb[i % 2], ps, gain[:, l:l + 1], r2[i % 2], ALU.mult, ALU.mult)

                def qk_p2(dst, i):
                    mm(banks[3][:], Rm, qnb[i % 2], True, True)
                    tt("dve", t1[i % 2], qnb[i % 2], rope[:, 0, tok], ALU.mult)
                    tt("dve", t2[i % 2], banks[3][:], rope[:, 1, tok], ALU.mult)
                    tt("pool", dst, t1[i % 2], t2[i % 2], ALU.add)

                def v_post():
                    cp("act", vTb, banks[4][:])
                    for j in range(4):
                        tr(tpb[:, j * 128:(j + 1) * 128], vTb[:, j * 128:(j + 1) * 128], identb)
                    for w in range(2):
                        cp("act", vz[:, 1 + 4 * g:5 + 4 * g, w, 64 * w:64 * w + 64],
                           tpb[:, 0:512].rearrange("p (j c) -> p j c", j=4, c=128)[:, :, 64 * w:64 * w + 64])

                def u_tile(tq, bk):
                    for kc in range(NKC):
                        mm(bk[:], hT[:, kc, tq * 128:(tq + 1) * 128], win[:, kc, 0:512], kc == 0, kc == NKC - 1)
                    cp("act", usb[tq % 2], bk[:])
                    r0 = (4 * g + tq) * 128
                    dma("sp", ub[r0:r0 + 128, :], usb[tq % 2])

                kdst = kT[:, 128 + g * 512:128 + (g + 1) * 512]
                proj(banks[1], 1024)
                proj(banks[4], 1152)
                proj(banks[5], 512)
                if l == 0 and g + 1 < 4:
                    for t_ in range(4 * (g + 1), 4 * (g + 2)):
                        x_tile(t_)
                if l == 0 and g == 2:
                    attn_consts()
                    late_params()
                if g + 1 < 4:
                    normA(g + 1)
                qk_p1(banks[1][:], gk, 0)
                proj(banks[6], 640)
                qk_p2(kdst, 0)
                v_post()
                qk_p1(banks[5][:], gq, 1)
                u_tile(0, banks[1])
                qk_p2(qT[:, 0, tok], 1)
                if g + 1 < 4:
                    normB(g + 1)
                proj(banks[5], 768)
                qk_p1(banks[6][:], gq, 2)
                u_tile(1, banks[4])
                qk_p2(qT[:, 1, tok], 2)
                proj(banks[6], 896)
                qk_p1(banks[5][:], gq, 3)
                u_tile(2, banks[1])
                qk_p2(qT[:, 2, tok], 3)
                qk_p1(banks[6][:], gq, 4)
                u_tile(3, banks[4])
                qk_p2(qT[:, 3, tok], 4)
                if g == 0 or g == 3:
                    fl = 0 if g == 0 else 1
                    blk = 1 if g == 0 else 16
                    dma("sp", kvb[fl * 128:(fl + 1) * 128, 0:128], kT[:, blk * 128:(blk + 1) * 128])
                    for w in range(2):
                        dma("sp", kvb[fl * 128:(fl + 1) * 128, 128 + 64 * w:192 + 64 * w],
                            vz[:, blk, w, 64 * w:64 * w + 64])
            allgather(kvg, kvb)
            allgather(ug, ub)
            for (blk, r0) in ((0, 128), (17, 256)):
                dma("sp", kT[:, blk * 128:(blk + 1) * 128], kvg[r0:r0 + 128, 0:128])
                for w in range(2):
                    dma("sp", vz[:, blk, w, 64 * w:64 * w + 64], kvg[r0:r0 + 128, 128 + 64 * w:192 + 64 * w])

            if l == 0:
                act(esink[:], esink[:], AF.Exp)
            norder = list(range(1, 15)) + [0, 15]
            npos = {n: i for i, n in enumerate(norder)}
            PTb = [c16(512 * i, [512]) for i in range(12)]
            acc_num, acc_den = banks[0], banks[1]
            st_banks = [banks[2], banks[3], banks[4], banks[5], banks[6]]
            msb = banks[7]
            gctr = [0]

            def tile_scores(n):
                par = npos[n] % 2
                for it in range(6):
                    mi, w = it // 2, it % 2
                    b_ = n + mi
                    st = st_banks[gctr[0] % 5]
                    gctr[0] += 1
                    pt = PTb[par * 6 + it]
                    mm(st[:], kT[64 * w:64 * w + 64, b_ * 128:(b_ + 1) * 128],
                       qT[64 * w:64 * w + 64, :, n * 128:(n + 1) * 128], True, True)
                    act(pt, st[:], AF.Exp, scale=0.125)
                    if mi != 1:
                        if mi == 0:
                            mk = masks[:, 2, :] if n == 0 else masks[:, 0, :]
                        else:
                            mk = masks[:, 3, :] if n == 15 else masks[:, 1, :]
                        tt("pool" if mi == 0 else "dve", pt, pt, mk, ALU.mult)

            def tile_pv(n):
                par = npos[n] % 2
                for it in range(6):
                    mi, w = it // 2, it % 2
                    b_ = n + mi
                    mm(acc_num[:], vz[:, b_, w, :], PTb[par * 6 + it], it == 0, it == 5)
                for it in range(6):
                    mi, w = it // 2, it % 2
                    mm(acc_den[:], onesz[:, w, :], PTb[par * 6 + it], it == 0, it == 5)

            esx = c32(5888, [512])
            for i in range(4):
                act(esx[:, i * 128:(i + 1) * 128], rope[:, 0, 0:128], AF.Identity, bias=esink[:, l, i:i + 1], scale=0.0)

            def tile_fin_a1(n):
                par = npos[n] % 2
                p3 = npos[n] % 3
                tt("dve", rden[p3], acc_den[:], esx, ALU.add)
                cp("dve", yA[p3], acc_num[:])

            def tile_fin_a2(n):
                par = npos[n] % 2
                rd, ya, sq = rden[npos[n] % 3], yA[npos[n] % 3], sqb[par]
                act(rd, rd, AF.Ln)
                act(rd, rd, AF.Exp, scale=-1.0)
                tt("dve", ya, ya, rd, ALU.mult)
                tt("pool", sq, ya, ya, ALU.mult)

            def tile_fin_b1(n):
                par = npos[n] % 2
                sq = sqb[par]
                for i in range(4):
                    mm(msb[:, 0:128], o512, sq[:, i * 128:(i + 1) * 128], i == 0, i == 3)

            def tile_fin_b2(n):
                par = npos[n] % 2
                ya = yA[npos[n] % 3]
                rsqrt_eps(rA[par], msb[:, 0:128])
                for i in range(4):
                    stt("dve", mixA[:, i, :].rearrange("p (ka kpl) -> p kpl ka", kpl=64)[:, 4 * n:4 * n + 4, :],
                        ya[:, i * 128:(i + 1) * 128].rearrange("p (a b) -> p a b", a=4, b=32), gA[:, l, i:i + 1],
                        rA[par].rearrange("p (a b) -> p a b", a=4, b=32), ALU.mult, ALU.mult)

            tile_scores(norder[0])
            for t in range(16):
                n = norder[t]
                if t >= 2:
                    tile_fin_b1(norder[t - 2])
                if t + 1 < 16:
                    tile_scores(norder[t + 1])
                if t >= 1:
                    tile_fin_a2(norder[t - 1])
                tile_pv(n)
                tile_fin_a1(n)
                if t >= 2:
                    tile_fin_b2(norder[t - 2])
            tile_fin_a2(norder[15])
            for t_ in (14, 15):
                tile_fin_b1(norder[t_])
                tile_fin_b2(norder[t_])

            Ua = [c16(2048 * i, [2048]) for i in range(2)]
            Ysb = [c16(25344, [2048]), c16(27392, [2048]), c16(34560, [2048]), c16(36608, [2048])]
            ccs = c16(16400, [2, 4, 512])
            wf = c16(20496, [4, 512])
            Yl = [c16(22544, [4, 2, 512]), c16(26640, [4, 2, 512])]
            Tl = [c16(30736, [4, 2, 128]), c16(31760, [4, 2, 128])]
            PQ = c16(32784, [2, 4, KG])
            Zt = c16(34832, [4, KG])
            mixF = c16(35856, [4, KG])
            sqf = [c16(36880, [KG]), c16(37136, [KG])]
            wo = c16(4096, [8, 1024])
            yF = c32(0, [4, KG])
            rF = c32(1024, [KG])

            dma("sp", ccs, ccs_d)
            dma("pool", wf, w_f[l].rearrange("(c p) n -> p c n", p=128))

            dma("pool", wo[:, 0:4, :], w_o[l, 0:512, :].rearrange("(c p) n -> p c n", p=128))
            for w in range(2):
                dma("pool", wo[64 * w:64 * w + 64, 4:8, :],
                    w_o[l, 512 + 256 * w:512 + 256 * w + 256, :].rearrange("(i d) n -> d i n", d=64))
            ugv = ug.rearrange("(a p) c -> a (p c)", p=128)
            def s1_load(sc):
                for j in range(4):
                    dma("sp", Ua[sc % 2][32 * j:32 * j + 32, :],
                        ugv[:, 16384 * j + 2048 * sc:16384 * j + 2048 * (sc + 1)])

            s1_load(0)
            for sc in range(8):
                if sc + 1 < 8:
                    s1_load(sc + 1)
                ysb = Ysb[sc % 4]
                for m in range(4):
                    bk = banks[(sc % 2) * 4 + m]
                    mm(bk[:], wbd[:, :], Ua[sc % 2][:, m * 512:(m + 1) * 512], True, True)
                    cp("act" if m % 2 == 0 else "dve", ysb[:, m * 512:(m + 1) * 512], bk[:])
                for j in range(4):
                    c_ = 16384 * j + 2048 * sc
                    dma("act", ys[:, c_:c_ + 2048], ysb[32 * j:32 * j + 32, :])

            mcs = c16(12288, [2, 4, 512])
            for ri in range(2):
                for cch in range(4):
                    bk = banks[4 + (ri * 4 + cch) % 4]
                    for c2 in range(4):
                        mm(bk[:], ccs[:, ri, c2, cch * 128:(cch + 1) * 128], wf[:, c2, :], c2 == 0, c2 == 3)
                    cp("act" if cch % 2 == 0 else "dve", mcs[:, ri, cch, :], bk[:])

            def stage2(g, ccs_=None):
                yl = Yl[g % 2]
                tl = Tl[g % 2]
                if ccs_ is None or 0 in ccs_:
                    if g < 4:
                        for ri in range(2):
                            r0_ = ri * 16 + 4 * g
                            dma("sp", yl[:, :, ri, :], ys[r0_:r0_ + 4, :].rearrange("j (p c) -> p j c", p=128))
                    else:
                        for j in range(4):
                            ka_ = 4 * g + j
                            rrow = ka_ if ka_ <= 16 else 32 - ka_
                            irow = 48 - ka_ if ka_ >= 17 else 17
                            dma("sp", yl[:, j, 0, :], ys[rrow:rrow + 1, :].rearrange("o (p c) -> p (o c)", p=128))
                            dma("sp", yl[:, j, 1, :], ys[irow:irow + 1, :].rearrange("o (p c) -> p (o c)", p=128))
                    dma("sp", tl, t2_d[g])
                for cc in (range(4) if ccs_ is None else ccs_):
                    for j in range(4):
                        o_ = banks[cc][:, j * 128:(j + 1) * 128]
                        mm(o_, yl[:, j, 0, cc * 128:(cc + 1) * 128], tl[:, j, 0, :], True, False)
                        mm(o_, yl[:, j, 1, cc * 128:(cc + 1) * 128], tl[:, j, 1, :], False, True)

            def evac_group():
                for cc in range(4):
                    bv = banks[cc][:].rearrange("p (j x) -> p j x", j=4, x=128)
                    e_ = "act" if cc % 2 == 0 else "dve"
                    cp(e_, PQ[:, 0, cc, :].rearrange("p (j x) -> p j x", j=4, x=64), bv[:, :, 0:64])
                    cp(e_, PQ[:, 1, cc, :].rearrange("p (j x) -> p j x", j=4, x=64), bv[:, :, 64:128])

            mixFb = [mixF, Zt]

            def wo_part(g, gi_, dcs):
                mf = mixFb[gi_ % 2]
                for dc in dcs:
                    ob = banks[6 + (dc % 2)][:, KG:2 * KG]
                    for i in range(4):
                        mm(ob, wo[:, i, dc * 128:(dc + 1) * 128], mf[:, i, :], i == 0, False)
                    for i in range(4):
                        mm(ob, wo[:, 4 + i, dc * 128:(dc + 1) * 128], mixA[:, i, g * KG:(g + 1) * KG], False, i == 3)
                    xv = xT[:, dc, :].rearrange("p (kpl ka) -> p ka kpl", ka=32)[:, 4 * g:4 * g + 4, :]
                    tt("dve", xv, ob.rearrange("p (j x) -> p j x", j=4, x=64), xv, ALU.add)

            def post_group(g, gi_, nxt=None, prev=None):
                def fill(ccs2):
                    if nxt is not None:
                        stage2(nxt, ccs2)
                if prev is not None:
                    wo_part(prev, gi_ - 1, range(0, 4))
                for c3 in range(4):
                    yb = banks[4 + (c3 % 2)][:, 0:KG]
                    for cc in range(4):
                        mm(yb, mcs[:, 0, cc, c3 * 128:(c3 + 1) * 128], PQ[:, 0, cc, :], cc == 0, False)
                        mm(yb, mcs[:, 1, cc, c3 * 128:(c3 + 1) * 128], PQ[:, 1, cc, :], False, cc == 3)
                    act(yF[:, c3, :], yb, AF.Identity, bias=bF[:, l, c3:c3 + 1])
                fill([0, 1])
                for c3 in range(4):
                    act(sqf[c3 % 2], yF[:, c3, :], AF.Square)
                    mm(banks[5][:, KG:2 * KG], o512, sqf[c3 % 2], c3 == 0, c3 == 3)
                if prev is not None:
                    wo_part(prev, gi_ - 1, range(4, 8))
                rsqrt_eps(rF, banks[5][:, KG:2 * KG])
                mf = mixFb[gi_ % 2]
                for c3 in range(4):
                    stt("dve", mf[:, c3, :], yF[:, c3, :], gF[:, l, c3:c3 + 1], rF, ALU.mult, ALU.mult)
                fill([2, 3])

            sqh = c16(37392, [8, 2])
            hh = c16(37408, [8, 2])
            hstage = c16(37440, [8, 2])
            rh = c32(1280, [2])

            def early_halo():
                for kc in range(NKC):
                    act(sqh[:, kc, :], xT[:, kc, 0:2048:2047], AF.Square)
                    mm(banks[5][:, 0:2], o1024, sqh[:, kc, :], kc == 0, kc == NKC - 1)
                rsqrt_eps(rh, banks[5][:, 0:2])
                for kc in range(NKC):
                    stt("dve", hh[:, kc, :], xT[:, kc, 0:2048:2047], g2[:, l, kc:kc + 1], rh, ALU.mult, ALU.mult)
                dma("pool", hb[0:1, :].rearrange("o (c p) -> p (o c)", p=128), hh[:, :, 0])
                dma("pool", hb[1:2, :].rearrange("o (c p) -> p (o c)", p=128), hh[:, :, 1])
                allgather(hg, hb)
                dma("pool", hstage[:, :, 0], hg[1:2, :].rearrange("o (c p) -> p (o c)", p=128))
                dma("pool", hstage[:, :, 1], hg[2:3, :].rearrange("o (c p) -> p (o c)", p=128))

            pairs = [(grp, fh, jp) for grp in range(2) for fh in range(2) for jp in range(11)]
            wup = [c16(16400 + 2048 * i, [8, 256]) for i in range(3)]

            def load_wup(q):
                grp, fh, jp = pairs[q]
                fc = fh * 11 + jp
                wu = wup[q % 3]
                for wh in range(2):
                    col = wh * DFF + fc * 128
                    dma("pool", wu[:, :, wh * 128:(wh + 1) * 128],
                        w_up[l, :, col:col + 128].rearrange("(c p) n -> p c n", p=128))

            gorder = [7, 0, 1, 2, 3, 4, 5, 6]
            stage2(gorder[0])
            for gi, kg in enumerate(gorder):
                evac_group()
                post_group(kg, gi, gorder[gi + 1] if gi + 1 < NKG else None, gorder[gi - 1] if gi >= 1 else None)
                if gi == 2:
                    early_halo()
                    load_wup(0)
                    load_wup(1)
            wo_part(gorder[-1], NKG - 1, range(0, 8))

            hT2 = c16(0, [8, 2050])
            actT = c16(22544, [11, 1024])
            wdn = c16(33808, [11, 1024])
            wup = [c16(16400 + 2048 * i, [8, 256]) for i in range(3)]
            sq2 = [c16(45072, [512]), c16(45584, [512])]
            sg = c16(46096, [1024])
            xs = [[c32(0, [1026]), c32(1026, [1026])], [c32(2052, [1026]), c32(3078, [1026])]]
            cv = [c32(4104, [1024]), c32(5128, [1024])]
            rstd2 = c32(6152, [504]) if False else None

            for g in range(4):
                rmsnorm_T(l, g, g2, lambda kc: hT2[:, kc, 1 + g * 512:1 + (g + 1) * 512], sq2, cv[0][:, 0:512],
                          banks[g % 2])
            ts("dve", hT2[:, :, 0], hstage[:, :, 0], hmask[:, 0:1], None, ALU.mult)
            ts("dve", hT2[:, :, 2049], hstage[:, :, 1], hmask[:, 1:2], None, ALU.mult)

            for q, (grp, fh, jp) in enumerate(pairs):
                c0 = 1 + 1024 * grp
                if jp == 0:
                    dma("pool", wdn, w_down[l, fh * 1408:(fh + 1) * 1408, :].rearrange("(j p) n -> p j n", p=128))
                if q + 2 < len(pairs):
                    load_wup(q + 2)
                fc = fh * 11 + jp
                wu = wup[q % 3]
                for wh in range(2):
                    ccol = wh * 22 + fc
                    xb = xs[q % 2][wh]
                    ub_ = [banks[2 * wh], banks[2 * wh + 1]]
                    hbk = banks[4 + wh]
                    for hf in range(2):
                        for kc in range(NKC):
                            mm(ub_[hf][:], wu[:, kc, wh * 128:(wh + 1) * 128],
                               hT2[:, kc, c0 + hf * 512:c0 + (hf + 1) * 512], kc == 0, kc == NKC - 1)
                    for kc in range(NKC):
                        mm(hbk[:, 0:2], wu[:, kc, wh * 128:(wh + 1) * 128],
                           hT2[:, kc, c0 - 1:c0 + 1025:1025], kc == 0, kc == NKC - 1)
                    for hf in range(2):
                        cp("act", xb[:, 1 + hf * 512:513 + hf * 512], ub_[hf][:])
                        act(cv[wh][:, hf * 512:(hf + 1) * 512], ub_[hf][:], AF.Identity,
                            bias=cb[:, l, ccol:ccol + 1], scale=cw[:, l, 1, ccol:ccol + 1])
                    cp("act", xb[:, 0:1026:1025], hbk[:, 0:2])
                    stt("dve", cv[wh], xb[:, 0:1024], cw[:, l, 0, ccol:ccol + 1], cv[wh], ALU.mult, ALU.add)
                    stt("dve", cv[wh], xb[:, 2:1026], cw[:, l, 2, ccol:ccol + 1], cv[wh], ALU.mult, ALU.add)
                act(sg, cv[0], AF.Silu)
                tt("pool", actT[:, jp, :], sg, cv[1], ALU.mult)
                if jp == 10:
                    last = (l == depth - 1 and grp == 1 and fh == 1)
                    if not last:
                        for dc in range(NKC):
                            for hf in range(2):
                                db = banks[6 + ((dc * 2 + hf) % 2)]
                                for j2 in range(11):
                                    mm(db[:], wdn[:, j2, dc * 128:(dc + 1) * 128],
                                       actT[:, j2, hf * 512:(hf + 1) * 512], j2 == 0, j2 == 10)
                                tk = slice(grp * 1024 + hf * 512, grp * 1024 + (hf + 1) * 512)
                                tt("dve", xT[:, dc, tk], db[:], xT[:, dc, tk], ALU.add)
                    else:
                        pend = list(range(8))
                        for hf in range(2):
                            for dc in range(NKC):
                                db = banks[6 + (dc % 2)]
                                for j2 in range(11):
                                    mm(db[:], wdn[:, j2, dc * 128:(dc + 1) * 128],
                                       actT[:, j2, hf * 512:(hf + 1) * 512], j2 == 0, j2 == 10)
                                tk = slice(grp * 1024 + hf * 512, grp * 1024 + (hf + 1) * 512)
                                tt("dve", xT[:, dc, tk], db[:], xT[:, dc, tk], ALU.add)
                                if pend and (hf == 1 or dc % 2 == 1):
                                    writeback(pend.pop(0))
                            if hf == 0:
                                pend += [8, 9, 10, 11]

        for t in range(16):
            if t not in wb_done:
                writeback(t)

        P.emit(sem_c)
    return nc


def _consts(h):
    bf = ml_dtypes.bfloat16
    a = np.arange(32, dtype=np.float64)
    ang1 = 2.0 * np.pi * (a[:, None] * a[None, :]) / 32.0
    w32 = np.concatenate([np.cos(ang1), np.sin(ang1)], axis=1)
    w32r = np.concatenate([np.cos(ang1)[:, 0:17], np.sin(ang1)[:, 1:16]], axis=1)
    wbd = np.zeros((4, 32, 4, 32), np.float64)
    for j in range(4):
        wbd[j, :, j, :] = w32r
    wbd = wbd.reshape(128, 128).astype(bf)
    p = np.arange(128, dtype=np.int64)
    ka = np.arange(32, dtype=np.int64)
    kpl = np.arange(64, dtype=np.int64)
    kk = ka[:, None] + 32 * (kpl[None, :] + 64 * h)
    th = 2.0 * np.pi * ((p[:, None, None] * kk[None, :, :]) % SEQ).astype(np.float64) / SEQ
    T1 = np.concatenate([np.cos(th), np.sin(th)], axis=2)
    T2 = np.concatenate([-np.sin(th), np.cos(th)], axis=2)
    sgn = np.where(ka > 16, -1.0, 1.0) * np.where((ka == 0) | (ka == 16), 0.0, 1.0)
    T2 = T2 * sgn[None, :, None]
    t2 = np.stack([T1, T2], axis=2)
    t2 = t2.reshape(128, NKG, 4, 2, 128).transpose(1, 0, 2, 3, 4)
    t2 = np.ascontiguousarray(t2).astype(bf)
    c = np.arange(512, dtype=np.int64)
    a2 = 2.0 * np.pi * ((c[:, None] * c[None, :]) % 512).astype(np.float64) / 512
    scale = 1.0 / np.sqrt(float(SEQ) * 512.0)
    cc = np.stack([np.cos(a2) * scale, -np.sin(a2) * scale], axis=0)
    ccs = cc.reshape(2, 4, 128, 512).transpose(2, 0, 1, 3)
    ccs = np.ascontiguousarray(ccs).astype(bf)
    inv = 1.0 / (10000.0 ** (np.arange(0, 64, 2, dtype=np.float32) / 64.0))
    pos = (2048 * h + np.arange(T)).astype(np.float32)
    angr = pos[None, :] * inv[:, None].astype(np.float32)
    idx = (np.arange(128) % 64) % 32
    rope = np.stack([np.cos(angr)[idx], np.sin(angr)[idx]], axis=1)
    rope = np.ascontiguousarray(rope).astype(bf)
    mats = np.zeros((128, 5, 128), np.float32)
    for m in range(128):
        d = m % 64
        if d < 32:
            mats[m + 32, 0, m] = -1.0
        else:
            mats[m - 32, 0, m] = 1.0
    mats[:64, 1, :64] = 1.0 / 64
    mats[64:, 1, 64:] = 1.0 / 64
    mats[:, 2, :] = 1.0 / 1024
    mats[:, 3, :] = 1.0 / 512
    mats[:, 4, :] = np.eye(128)
    jj = np.arange(128)[:, None]
    qi = np.arange(128)[None, :]
    prev = (jj >= qi).astype(np.float32)
    nxt = (jj <= qi).astype(np.float32)
    zero = np.zeros_like(prev)
    mk = np.stack([prev, nxt, prev if h == 1 else zero, nxt if h == 0 else zero], axis=1)
    masks = np.tile(mk[:, :, None, :], (1, 1, 4, 1)).reshape(128, 4, 512)
    onesz = np.zeros((128, 2, 128), np.float32)
    onesz[:, 0, :64] = 1.0
    onesz[:, 1, 64:] = 1.0
    hmask = np.zeros((128, 2), np.float32)
    hmask[:, 0] = 1.0 if h == 1 else 0.0
    hmask[:, 1] = 1.0 if h == 0 else 0.0
    return {
        "c_wbd": wbd, "c_t2": t2, "c_ccs": ccs, "c_rope": rope, "c_mats": mats.astype(bf),
        "c_identf": np.eye(128, dtype=np.float32), "c_masks": masks.astype(bf),
        "c_onesz": onesz.astype(bf), "c_hmask": hmask,
    }


_NC_CACHE = {}


def kernel(x, norm1, w_in, w_fourier, b_fourier, q_norm, k_norm, sink, g_fourier_out, g_attn_out, w_o,
           norm2, w_up, conv_w, conv_b, w_down):
    f = lambda a: np.ascontiguousarray(np.asarray(a, dtype=np.float32))
    x = f(x)
    shared = {
        "norm1": f(norm1), "norm2": f(norm2), "w_in": f(w_in), "w_fourier": f(w_fourier),
        "b_fourier": f(b_fourier), "q_norm": f(q_norm), "k_norm": f(k_norm), "sink": f(sink),
        "g_fourier_out": f(g_fourier_out), "g_attn_out": f(g_attn_out), "w_o": f(w_o),
        "w_up": f(w_up), "conv_w": f(conv_w), "conv_b": f(conv_b), "w_down": f(w_down),
    }
    consts = [_consts(0), _consts(1)]
    if "nc" not in _NC_CACHE:
        _NC_CACHE["nc"] = build_nc()
    nc = _NC_CACHE["nc"]
    in_maps = []
    for c in range(8):
        b, h = c // 2, c % 2
        m = dict(shared)
        m.update(consts[h])
        m["x"] = np.ascontiguousarray(x[b, h * T:(h + 1) * T, :])
        in_maps.append(m)
    res = run_bass_kernel_spmd(nc, in_maps, core_ids=list(range(8)))
    out = np.empty((4, SEQ, D), np.float32)
    for c in range(8):
        b, h = c // 2, c % 2
        out[b, h * T:(h + 1) * T, :] = res.results[c]["y"]
    return out
```

```python
import numpy as np
import ml_dtypes
import concourse.bass as bass
import concourse.mybir as mybir
from concourse.bass_utils import run_bass_kernel_spmd

F32, BF16 = mybir.dt.float32, mybir.dt.bfloat16
AF = mybir.ActivationFunctionType
ALU = mybir.AluOpType

D = 1024
T = 2048
SEQ = 4096
DEPTH = 2
DFF = 2816
EPS = 1e-6
NKC = 8
KG = 256
NKG = T // KG
RG = [[0, 1], [2, 3], [4, 5], [6, 7]]


def _rect(ap, whole=False):
    name = ap.tensor.name
    dims = ap.ap
    off = ap.offset
    sp = str(ap.space)
    if "DRAM" in sp.upper() or "HBM" in sp.upper():
        shp = tuple(ap.tensor.shape)
        if len(shp) == 2:
            C = int(shp[1])
            rext = 0
            cext = 0
            for st_, c in dims:
                st_ = abs(int(st_))
                if st_ % C == 0:
                    rext += (c - 1) * (st_ // C)
                else:
                    cext += (c - 1) * st_
            r0, c0 = off // C, off % C
            if c0 + cext < C:
                return (name, r0, r0 + rext + 1, c0, c0 + cext + 1)
        ext = sum((c - 1) * abs(s) for s, c in dims) + 1
        return (name, 0, 1 << 30, off, off + ext)
    if "PSUM" in sp.upper():
        return (name, 0, 128, 0, 1 << 30)
    pstep, pcnt = dims[0]
    if pstep == 0:
        return (name, 0, 128, 0, 1 << 30)
    p0 = off // pstep
    f0 = off % pstep
    ext = sum((c - 1) * abs(s) for s, c in dims[1:]) + 1
    return (name, p0, p0 + pcnt, f0, f0 + ext)


class Prog:
    ENG = ("pe", "act", "dve", "pool", "sp")

    def __init__(self, nc):
        self.nc = nc
        self.ops = []
        self.track = {}

    def add(self, eng, fn, reads=(), writes=(), dma=False, inc=16):
        oid = len(self.ops)
        deps = set()
        rr = [_rect(a) for a in reads]
        wr = [_rect(a) for a in writes]
        wr = wr + [r for r in rr if r[0].startswith("bank")]
        rr = [r for r in rr if not r[0].startswith("bank")]
        for (name, p0, p1, f0, f1) in rr:
            for rec in self.track.get(name, ()):
                if rec[5] and rec[0] < p1 and p0 < rec[1] and rec[2] < f1 and f0 < rec[3]:
                    deps.add(rec[4])
        for (name, p0, p1, f0, f1) in wr:
            lst = self.track.get(name, [])
            keep = []
            for rec in lst:
                if rec[0] < p1 and p0 < rec[1] and rec[2] < f1 and f0 < rec[3]:
                    deps.add(rec[4])
                    if p0 <= rec[0] and rec[1] <= p1 and f0 <= rec[2] and rec[3] <= f1:
                        continue
                keep.append(rec)
            self.track[name] = keep
        for (name, p0, p1, f0, f1) in rr:
            self.track.setdefault(name, []).append((p0, p1, f0, f1, oid, False))
        for (name, p0, p1, f0, f1) in wr:
            self.track.setdefault(name, []).append((p0, p1, f0, f1, oid, True))
        deps.discard(oid)
        self.ops.append(dict(eng=eng, fn=fn, deps=deps, dma=dma, inc=inc, id=oid))
        return oid

    def emit(self, sems):
        ops = self.ops
        has_dep = [False] * len(ops)
        for o in ops:
            for d in o["deps"]:
                has_dep[d] = True
        cnt = {e: 0 for e in self.ENG}
        dcnt = {}
        dq_i = {e: 0 for e in self.ENG}
        last_on_sem = {}
        for o in ops:
            e = o["eng"]
            o["pre"] = None
            if o["dma"]:
                qn = "cc" if o["inc"] != 16 else e
                pool = sems["dma_" + qn]
                s = pool[dq_i.setdefault(qn, 0) % len(pool)]
                dq_i[qn] += 1
                key = id(s)
                o["pre"] = last_on_sem.get(key)
                v = dcnt.get(key, 0) + o["inc"]
                dcnt[key] = v
                o["done"] = (s, v)
                last_on_sem[key] = (s, v)
                o["signal"] = True
            else:
                if has_dep[o["id"]]:
                    cnt[e] += 1
                    o["signal"] = True
                else:
                    o["signal"] = False
                o["done"] = (sems[e], cnt[e])
        per_eng = {e: [] for e in self.ENG}
        waited = {e: {} for e in self.ENG}
        for o in ops:
            e = o["eng"]
            w = {}
            if o["pre"] is not None:
                s, v = o["pre"]
                w[id(s)] = (s, v)
            for d in o["deps"]:
                po = ops[d]
                if (not po["dma"]) and po["eng"] == e and e == "pe":
                    continue
                s, v = po["done"]
                if id(s) not in w or w[id(s)][1] < v:
                    w[id(s)] = (s, v)
            wl = []
            for k, (s, v) in w.items():
                if waited[e].get(k, 0) < v:
                    waited[e][k] = v
                    wl.append((s, v))
            per_eng[e].append((wl, o))
        nc = self.nc
        with nc.Block() as block:
            def run(engine, lst):
                for wl, o in lst:
                    emb = None
                    if wl:
                        emb = wl[-1]
                        wl = wl[:-1]
                    for s, v in wl:
                        engine.wait_ge(s, v)
                    ins = o["fn"](engine)
                    if emb is not None:
                        ins._wait_ge(emb[0], emb[1])
                    if o["signal"]:
                        s, v = o["done"]
                        if o["dma"]:
                            if o["inc"] == 16:
                                ins.then_inc(s, 16)
                            else:
                                ins.then_inc(s)
                        else:
                            ins.then_inc(s, 1)
                return

            @block.tensor
            def _(eng):
                run(eng, per_eng["pe"])

            @block.scalar
            def _(eng):
                run(eng, per_eng["act"])
                for s in sems.get("dma_act", []):
                    v = dcnt.get(id(s), 0)
                    if v:
                        eng.wait_ge(s, v)

            @block.vector
            def _(eng):
                run(eng, per_eng["dve"])

            @block.gpsimd
            def _(eng):
                run(eng, per_eng["pool"])
                for s in sems["dma_pool"] + sems.get("dma_cc", []):
                    v = dcnt.get(id(s), 0)
                    if v:
                        eng.wait_ge(s, v)

            @block.sync
            def _(eng):
                run(eng, per_eng["sp"])
                for s in sems["dma_sp"]:
                    v = dcnt.get(id(s), 0)
                    if v:
                        eng.wait_ge(s, v)


def build_nc(depth=DEPTH, stage="full"):
    nc = bass.Bass("TRN2", target_bir_lowering=False)
    P = Prog(nc)

    def din(name, shape, dt=F32):
        return nc.dram_tensor(name, list(shape), dt, kind="ExternalInput").ap()

    x = din("x", [T, D])
    norm1 = din("norm1", [DEPTH, D]); norm2 = din("norm2", [DEPTH, D])
    w_in = din("w_in", [DEPTH, D, 1280]); w_f = din("w_fourier", [DEPTH, 512, 512])
    b_f = din("b_fourier", [DEPTH, 512]); q_norm = din("q_norm", [DEPTH, 64]); k_norm = din("k_norm", [DEPTH, 64])
    sink = din("sink", [DEPTH, 8]); g_f = din("g_fourier_out", [DEPTH, 512]); g_a = din("g_attn_out", [DEPTH, 512])
    w_o = din("w_o", [DEPTH, D, D]); w_up = din("w_up", [DEPTH, D, 2 * DFF])
    conv_w = din("conv_w", [DEPTH, 3, 2 * DFF]); conv_b = din("conv_b", [DEPTH, 2 * DFF])
    w_down = din("w_down", [DEPTH, DFF, D])
    wbd_d = din("c_wbd", [128, 128], BF16)
    t2_d = din("c_t2", [NKG, 128, 4, 2, 128], BF16)
    ccs_d = din("c_ccs", [128, 2, 4, 512], BF16)
    rope_d = din("c_rope", [128, 2, T], BF16)
    mats_d = din("c_mats", [128, 5, 128], BF16)
    identf_d = din("c_identf", [128, 128], F32)
    masks_d = din("c_masks", [128, 4, 512], BF16)
    onesz_d = din("c_onesz", [128, 2, 128], BF16)
    hmask_d = din("c_hmask", [128, 2], F32)
    y = nc.dram_tensor("y", [T, D], F32, kind="ExternalOutput").ap()

    ub = nc.dram_tensor("ub", [T, 512], BF16).ap()
    ug = nc.dram_tensor("ug", [SEQ, 512], BF16).ap()
    kvb = nc.dram_tensor("kvb", [256, 256], BF16).ap()
    kvg = nc.dram_tensor("kvg", [512, 256], BF16).ap()
    ys = nc.dram_tensor("ys", [32, 65536], BF16).ap()
    hb = nc.dram_tensor("hb", [2, 1024], BF16).ap()
    hg = nc.dram_tensor("hg", [4, 1024], BF16).ap()

    A16 = 47616
    A32 = 6656
    import contextlib
    es = contextlib.ExitStack()
    with es:
        def sb(name, shape, dt):
            return es.enter_context(nc.sbuf_tensor(name, list(shape), dt))
        xT = sb("xT", [128, NKC, T], F32)
        a16 = sb("a16", [128, A16], BF16)
        a32 = sb("a32", [128, A32], F32)
        rope = sb("rope", [128, 2, T], BF16)
        mats = sb("mats", [128, 5, 128], BF16)
        identf = sb("identf", [128, 128], F32)
        masks = sb("masks", [128, 4, 512], BF16)
        onesz = sb("onesz", [128, 2, 128], BF16)
        hmask = sb("hmask", [128, 2], F32)
        epsc = sb("epsc", [128, 1], F32)
        wbd = sb("wbd", [128, 128], BF16)
        g1 = sb("g1", [128, DEPTH, 8], F32); g2 = sb("g2", [128, DEPTH, 8], F32)
        gq = sb("gq", [128, DEPTH], F32); gk = sb("gk", [128, DEPTH], F32)
        esink = sb("esink", [128, DEPTH, 4], F32)
        gF = sb("gF", [128, DEPTH, 4], F32); gA = sb("gA", [128, DEPTH, 4], F32)
        bF = sb("bF", [128, DEPTH, 4], F32)
        cw = sb("cw", [128, DEPTH, 3, 44], F32); cb = sb("cb", [128, DEPTH, 44], F32)
        banks = [es.enter_context(nc.psum_tensor(f"bank{i}", [128, 512], F32)) for i in range(8)]
        sem_c = {e: es.enter_context(nc.semaphore("s_" + e)) for e in ("pe", "act", "dve", "pool", "sp")}
        sem_c["dma_sp"] = [es.enter_context(nc.semaphore(f"dsp{i}")) for i in range(24)]
        sem_c["dma_pool"] = [es.enter_context(nc.semaphore(f"dpl{i}")) for i in range(24)]
        sem_c["dma_cc"] = [es.enter_context(nc.semaphore(f"dcc{i}")) for i in range(3)]
        sem_c["dma_act"] = [es.enter_context(nc.semaphore(f"dac{i}")) for i in range(8)]

        Rm, bones, o1024, o512, identb = (mats[:, i, :] for i in range(5))

        def c16(off, shape):
            n = int(np.prod(shape))
            assert off + n <= A16, (off, n)
            v = a16[:, off:off + n]
            if len(shape) == 2:
                v = v.rearrange("p (a b) -> p a b", a=shape[0], b=shape[1])
            elif len(shape) == 3:
                v = v.rearrange("p (a b c) -> p a b c", a=shape[0], b=shape[1], c=shape[2])
            return v

        def c32(off, shape):
            n = int(np.prod(shape))
            assert off + n <= A32, (off, n)
            v = a32[:, off:off + n]
            if len(shape) == 2:
                v = v.rearrange("p (a b) -> p a b", a=shape[0], b=shape[1])
            return v

        def dma(eng, out, in_, **kw):
            P.add(eng, lambda e: e.dma_start(out=out, in_=in_, **kw), reads=[in_], writes=[out], dma=True)

        def mm(out, lhsT, rhs, start, stop, skip=False):
            P.add("pe", lambda e: e.matmul(out, lhsT, rhs, start=start, stop=stop, skip_group_check=skip),
                  reads=[lhsT, rhs], writes=[out])

        def tr(out, in_, ident):
            P.add("pe", lambda e: e.transpose(out, in_, ident), reads=[in_, ident], writes=[out])

        def act(out, in_, func, bias=None, scale=1.0):
            rd = [in_] + ([bias] if bias is not None and not isinstance(bias, float) else []) + \
                 ([scale] if not isinstance(scale, float) else [])
            if bias is None:
                P.add("act", lambda e: e.activation(out=out, in_=in_, func=func, scale=scale), reads=rd, writes=[out])
            else:
                P.add("act", lambda e: e.activation(out=out, in_=in_, func=func, bias=bias, scale=scale),
                      reads=rd, writes=[out])

        def ts(eng, out, in0, s1, s2, op0, op1=None):
            rd = [in0] + [s for s in (s1, s2) if s is not None and not isinstance(s, float)]
            if op1 is None:
                P.add(eng, lambda e: e.tensor_scalar(out=out, in0=in0, scalar1=s1, scalar2=None, op0=op0),
                      reads=rd, writes=[out])
            else:
                P.add(eng, lambda e: e.tensor_scalar(out=out, in0=in0, scalar1=s1, scalar2=s2, op0=op0, op1=op1),
                      reads=rd, writes=[out])

        def stt(eng, out, in0, scalar, in1, op0, op1):
            rd = [in0, in1] + ([scalar] if not isinstance(scalar, float) else [])
            P.add(eng, lambda e: e.scalar_tensor_tensor(out=out, in0=in0, scalar=scalar, in1=in1, op0=op0, op1=op1),
                  reads=rd, writes=[out])

        def tt(eng, out, in0, in1, op):
            P.add(eng, lambda e: e.tensor_tensor(out=out, in0=in0, in1=in1, op=op), reads=[in0, in1], writes=[out])

        def cp(eng, out, in_):
            if eng == "act":
                act(out, in_, AF.Copy)
            else:
                P.add(eng, lambda e: e.tensor_copy(out=out, in_=in_), reads=[in_], writes=[out])

        def recip(out, in_):
            P.add("dve", lambda e: e.reciprocal(out=out, in_=in_), reads=[in_], writes=[out])

        def rsqrt_eps(out, in_):
            act(out, in_, AF.Ln, bias=epsc[:, 0:1])
            act(out, out, AF.Exp, scale=-0.5)

        def allgather(out, in_):
            P.add("pool", lambda e: e.collective_compute("AllGather", ALU.bypass, replica_groups=RG,
                                                         ins=[in_.opt()], outs=[out.opt()]),
                  reads=[in_], writes=[out], dma=True, inc=1)

        nc_ctx = nc.allow_non_contiguous_dma(reason="tiny parameter layouts")
        es.enter_context(nc_ctx)

        P.add("dve", lambda e: e.memset(epsc[:], EPS), writes=[epsc[:]])
        dma("sp", identf[:], identf_d)

        def late_consts():
            dma("sp", mats[:], mats_d)
            dma("sp", rope[:], rope_d)
            dma("sp", wbd[:], wbd_d)
            dma("sp", masks[:], masks_d)
            dma("sp", onesz[:], onesz_d)
            dma("sp", hmask[:], hmask_d)

        dma("pool", g1[:], norm1.rearrange("l (c p) -> p l c", p=128))
        for w in range(2):
            dma("pool", gq[64 * w:64 * w + 64, :], q_norm.rearrange("l d -> d l"))
            dma("pool", gk[64 * w:64 * w + 64, :], k_norm.rearrange("l d -> d l"))

        def late_params():
            dma("pool", g2[:], norm2.rearrange("l (c p) -> p l c", p=128))
            dma("pool", gF[:], g_f.rearrange("l (c p) -> p l c", p=128))
            dma("pool", bF[:], b_f.rearrange("l (c p) -> p l c", p=128))
            for w in range(2):
                for l_ in range(DEPTH):
                    dma("pool", gA[64 * w:64 * w + 64, l_, :],
                        g_a[l_:l_ + 1, 256 * w:256 * w + 256].rearrange("o (i d) -> d (o i)", d=64))
                    dma("pool", esink[64 * w:64 * w + 64, l_, :], sink[l_:l_ + 1, 4 * w:4 * w + 4].partition_broadcast(64))
            dma("pool", cb[:], conv_b.rearrange("l (c p) -> p l c", p=128))
            for l_ in range(DEPTH):
                for j_ in range(3):
                    dma("pool", cw[:, l_, j_, :], conv_w[l_, j_:j_ + 1, :].rearrange("o (c p) -> p (o c)", p=128))

        def load_win(l):
            win = c16(0, [8, 1280])
            wl = w_in[l]
            dma("pool", win[:, :, 1024:1280], wl[:, 1024:1280].rearrange("(c p) n -> p c n", p=128))
            for w in range(2):
                for kc in range(NKC):
                    dma("pool", win[:, kc, 512:1024].rearrange("p (j w d) -> p j w d", j=4, w=2, d=64)[:, :, w, :],
                        wl[kc * 128:(kc + 1) * 128, 512 + 256 * w:512 + 256 * w + 256].rearrange(
                            "p (j d) -> p j d", d=64))
            dma("pool", win[:, :, 0:512], wl[:, 0:512].rearrange("(c p) n -> p c n", p=128))

        xin = [c32(1024 * i, [1024]) for i in range(4)]
        xin_late = [c32(3584, [1024]), c32(4608, [1024])]
        win0 = c16(0, [8, 1280])

        def x_tile(t):
            xi = xin[t % 4] if t < 4 else xin_late[t % 2]
            xsrc = x[t * 128:(t + 1) * 128, :]
            if t < 4:
                dma("sp", xi, xsrc)
            else:
                P.add("sp", (lambda o_, i_: (lambda e: e.dma_start(out=o_, in_=i_)))(xi, xsrc),
                      reads=[xsrc, win0], writes=[xi], dma=True)
            for hf in range(2):
                bk = banks[(2 * t + hf) % 4] if t < 4 else banks[(2, 3, 6, 7)[(2 * t + hf) % 4]]
                for j in range(4):
                    kc = hf * 4 + j
                    tr(bk[:, j * 128:(j + 1) * 128], xi[:, kc * 128:(kc + 1) * 128], identf[:])
                cp("act" if hf == 0 else "dve", xT[:, hf * 4:hf * 4 + 4, t * 128:(t + 1) * 128],
                   bk[:].rearrange("p (a b) -> p a b", a=4, b=128))

        for t in range(4):
            x_tile(t)
        late_consts()
        load_win(0)

        def rmsnorm_T(l, g, gain, dst_fn, sq_bufs, rstd_buf, bank):
            tok = slice(g * 512, (g + 1) * 512)
            for kc in range(NKC):
                sq = sq_bufs[kc % 2]
                act(sq, xT[:, kc, tok], AF.Square)
                mm(bank[:], o1024, sq, kc == 0, kc == NKC - 1)
            rsqrt_eps(rstd_buf, bank[:])
            for kc in range(NKC):
                stt("dve", dst_fn(kc), xT[:, kc, tok], gain[:, l, kc:kc + 1], rstd_buf, ALU.mult, ALU.mult)

        xo = [c32(0, [1024]), c32(1024, [1024])]
        wb_done = []

        def writeback(t):
            xw = xo[t % 2]
            for hf in range(2):
                bk = banks[(2 * t + hf) % 4]
                for j in range(4):
                    kc = hf * 4 + j
                    tr(bk[:, j * 128:(j + 1) * 128], xT[:, kc, t * 128:(t + 1) * 128], identf[:])
                cp("act" if hf == 0 else "dve", xw[:, hf * 512:(hf + 1) * 512], bk[:])
            dma("sp", y[t * 128:(t + 1) * 128, :], xw)
            wb_done.append(t)

        for l in range(depth):
            win = c16(0, [8, 1280])
            qT = c16(10240, [4, T])
            kT = c16(18432, [2304])
            vz = c16(20736, [18, 2, 128])
            hTb = [c16(25344, [8, 512]), c16(34560, [8, 512])]
            sqb = [c16(29440, [512]), c16(29952, [512])]
            qnb = [c16(30464, [512]), c16(30976, [512])]
            usb = [c16(31488, [512]), c16(32000, [512])]
            vTb = c16(32512, [512])
            PT = [c16(33024 + 512 * i, [512]) for i in range(3)]
            mixA = c16(39424, [4, T])
            rstd = c32(0, [512])
            r2 = [c32(512, [512]), c32(1024, [512])]
            t1 = [c32(1536, [512]), c32(2048, [512])]
            t2 = [c32(2560, [512]), c32(3072, [512])]
            rden = [c32(3584, [512]), c32(4096, [512]), c32(1024, [512])]
            yA = [c32(4608, [512]), c32(5120, [512]), c32(512, [512])]
            rA = [c32(5632, [128]), c32(5760, [128])]

            if l > 0:
                load_win(l)
            P.add("pool", lambda e: e.memset(vz, 0.0), writes=[vz])
            if l == 0:
                late_params()

            tpb = banks[7][:].bitcast(BF16)

            def normA(g):
                tok_ = slice(g * 512, (g + 1) * 512)
                for kc in range(NKC):
                    sq = sqb[kc % 2]
                    act(sq, xT[:, kc, tok_], AF.Square)
                    mm(banks[0][:], o1024, sq, kc == 0, kc == NKC - 1)
                rsqrt_eps(rstd, banks[0][:])

            def normB(g):
                tok_ = slice(g * 512, (g + 1) * 512)
                for kc in range(NKC):
                    stt("dve", hTb[g % 2][:, kc, :], xT[:, kc, tok_], g1[:, l, kc:kc + 1], rstd, ALU.mult, ALU.mult)

            normA(0)
            normB(0)
            for g in range(4):
                tok = slice(g * 512, (g + 1) * 512)
                hT = hTb[g % 2]

                def proj(bank, c0):
                    for kc in range(NKC):
                        mm(bank[:], win[:, kc, c0:c0 + 128], hT[:, kc, :], kc == 0, kc == NKC - 1)

                def qk_p1(ps, gain, i):
                    sq = sqb[i % 2]
                    act(sq, ps, AF.Square)
                    mm(banks[2][:], bones, sq, True, True)
                    rsqrt_eps(r2[i % 2], banks[2][:])
                    stt("dve", qnb[i % 2], ps, gain[:, l:l + 1], r2[i % 2], ALU.mult, ALU.mult)

                def qk_p2(dst, i):
                    mm(banks[3][:], Rm, qnb[i % 2], True, True)
                    tt("dve", t1[i % 2], qnb[i % 2], rope[:, 0, tok], ALU.mult)
                    tt("dve", t2[i % 2], banks[3][:], rope[:, 1, tok], ALU.mult)
                    tt("pool", dst, t1[i % 2], t2[i % 2], ALU.add)

                def v_post():
                    cp("act", vTb, banks[4][:])
                    for j in range(4):
                        tr(tpb[:, j * 128:(j + 1) * 128], vTb[:, j * 128:(j + 1) * 128], identb)
                    for w in range(2):
                        cp("act", vz[:, 1 + 4 * g:5 + 4 * g, w, 64 * w:64 * w + 64],
                           tpb[:, 0:512].rearrange("p (j c) -> p j c", j=4, c=128)[:, :, 64 * w:64 * w + 64])

                def u_tile(tq, bk):
                    for kc in range(NKC):
                        mm(bk[:], hT[:, kc, tq * 128:(tq + 1) * 128], win[:, kc, 0:512], kc == 0, kc == NKC - 1)
                    cp("act", usb[tq % 2], bk[:])
                    r0 = (4 * g + tq) * 128
                    dma("sp", ub[r0:r0 + 128, :], usb[tq % 2])

                kdst = kT[:, 128 + g * 512:128 + (g + 1) * 512]
                proj(banks[1], 1024)
                proj(banks[4], 1152)
                proj(banks[5], 512)
                if l == 0 and g + 1 < 4:
                    for t_ in range(4 * (g + 1), 4 * (g + 2)):
                        x_tile(t_)
                if g + 1 < 4:
                    normA(g + 1)
                qk_p1(banks[1][:], gk, 0)
                proj(banks[6], 640)
                qk_p2(kdst, 0)
                v_post()
                qk_p1(banks[5][:], gq, 1)
                u_tile(0, banks[1])
                qk_p2(qT[:, 0, tok], 1)
                if g + 1 < 4:
                    normB(g + 1)
                proj(banks[5], 768)
                qk_p1(banks[6][:], gq, 2)
                u_tile(1, banks[4])
                qk_p2(qT[:, 1, tok], 2)
                proj(banks[6], 896)
                qk_p1(banks[5][:], gq, 3)
                u_tile(2, banks[1])
                qk_p2(qT[:, 2, tok], 3)
                qk_p1(banks[6][:], gq, 4)
                u_tile(3, banks[4])
                qk_p2(qT[:, 3, tok], 4)
                if g == 0 or g == 3:
                    fl = 0 if g == 0 else 1
                    blk = 1 if g == 0 else 16
                    dma("sp", kvb[fl * 128:(fl + 1) * 128, 0:128], kT[:, blk * 128:(blk + 1) * 128])
                    for w in range(2):
                        dma("sp", kvb[fl * 128:(fl + 1) * 128, 128 + 64 * w:192 + 64 * w],
                            vz[:, blk, w, 64 * w:64 * w + 64])
            allgather(kvg, kvb)
            allgather(ug, ub)
            for (blk, r0) in ((0, 128), (17, 256)):
                dma("sp", kT[:, blk * 128:(blk + 1) * 128], kvg[r0:r0 + 128, 0:128])
                for w in range(2):
                    dma("sp", vz[:, blk, w, 64 * w:64 * w + 64], kvg[r0:r0 + 128, 128 + 64 * w:192 + 64 * w])

            if l == 0:
                act(esink[:], esink[:], AF.Exp)
            norder = list(range(1, 15)) + [0, 15]
            npos = {n: i for i, n in enumerate(norder)}
            PTb = [c16(512 * i, [512]) for i in range(12)]
            acc_num, acc_den = banks[0], banks[1]
            st_banks = [banks[2], banks[3], banks[4], banks[5], banks[6]]
            msb = banks[7]
            gctr = [0]

            def tile_scores(n):
                par = npos[n] % 2
                for it in range(6):
                    mi, w = it // 2, it % 2
                    b_ = n + mi
                    st = st_banks[gctr[0] % 5]
                    gctr[0] += 1
                    pt = PTb[par * 6 + it]
                    mm(st[:], kT[64 * w:64 * w + 64, b_ * 128:(b_ + 1) * 128],
                       qT[64 * w:64 * w + 64, :, n * 128:(n + 1) * 128], True, True)
                    act(pt, st[:], AF.Exp, scale=0.125)
                    if mi != 1:
                        if mi == 0:
                            mk = masks[:, 2, :] if n == 0 else masks[:, 0, :]
                        else:
                            mk = masks[:, 3, :] if n == 15 else masks[:, 1, :]
                        tt("pool" if mi == 0 else "dve", pt, pt, mk, ALU.mult)

            def tile_pv(n):
                par = npos[n] % 2
                for it in range(6):
                    mi, w = it // 2, it % 2
                    b_ = n + mi
                    mm(acc_num[:], vz[:, b_, w, :], PTb[par * 6 + it], it == 0, it == 5)
                for it in range(6):
                    mi, w = it // 2, it % 2
                    mm(acc_den[:], onesz[:, w, :], PTb[par * 6 + it], it == 0, it == 5)

            esx = c32(5888, [512])
            for i in range(4):
                act(esx[:, i * 128:(i + 1) * 128], rope[:, 0, 0:128], AF.Identity, bias=esink[:, l, i:i + 1], scale=0.0)

            def tile_fin_a1(n):
                par = npos[n] % 2
                p3 = npos[n] % 3
                tt("dve", rden[p3], acc_den[:], esx, ALU.add)
                cp("dve", yA[p3], acc_num[:])

            def tile_fin_a2(n):
                par = npos[n] % 2
                rd, ya, sq = rden[npos[n] % 3], yA[npos[n] % 3], sqb[par]
                act(rd, rd, AF.Ln)
                act(rd, rd, AF.Exp, scale=-1.0)
                tt("dve", ya, ya, rd, ALU.mult)
                tt("pool", sq, ya, ya, ALU.mult)

            def tile_fin_b1(n):
                par = npos[n] % 2
                sq = sqb[par]
                for i in range(4):
                    mm(msb[:, 0:128], o512, sq[:, i * 128:(i + 1) * 128], i == 0, i == 3)

            def tile_fin_b2(n):
                par = npos[n] % 2
                ya = yA[npos[n] % 3]
                rsqrt_eps(rA[par], msb[:, 0:128])
                for i in range(4):
                    stt("dve", mixA[:, i, :].rearrange("p (ka kpl) -> p kpl ka", kpl=64)[:, 4 * n:4 * n + 4, :],
                        ya[:, i * 128:(i + 1) * 128].rearrange("p (a b) -> p a b", a=4, b=32), gA[:, l, i:i + 1],
                        rA[par].rearrange("p (a b) -> p a b", a=4, b=32), ALU.mult, ALU.mult)

            tile_scores(norder[0])
            for t in range(16):
                n = norder[t]
                if t >= 2:
                    tile_fin_b1(norder[t - 2])
                if t + 1 < 16:
                    tile_scores(norder[t + 1])
                if t >= 1:
                    tile_fin_a2(norder[t - 1])
                tile_pv(n)
                tile_fin_a1(n)
                if t >= 2:
                    tile_fin_b2(norder[t - 2])
            tile_fin_a2(norder[15])
            for t_ in (14, 15):
                tile_fin_b1(norder[t_])
                tile_fin_b2(norder[t_])

            Ua = [c16(2048 * i, [2048]) for i in range(2)]
            Ysb = [c16(25344, [2048]), c16(27392, [2048]), c16(34560, [2048]), c16(36608, [2048])]
            ccs = c16(16400, [2, 4, 512])
            wf = c16(20496, [4, 512])
            Yl = [c16(22544, [4, 2, 512]), c16(26640, [4, 2, 512])]
            Tl = [c16(30736, [4, 2, 128]), c16(31760, [4, 2, 128])]
            PQ = c16(32784, [2, 4, KG])
            Zt = c16(34832, [4, KG])
            mixF = c16(35856, [4, KG])
            sqf = [c16(36880, [KG]), c16(37136, [KG])]
            wo = c16(4096, [8, 1024])
            yF = c32(0, [4, KG])
            rF = c32(1024, [KG])

            dma("sp", ccs, ccs_d)
            dma("pool", wf, w_f[l].rearrange("(c p) n -> p c n", p=128))

            dma("pool", wo[:, 0:4, :], w_o[l, 0:512, :].rearrange("(c p) n -> p c n", p=128))
            for w in range(2):
                dma("pool", wo[64 * w:64 * w + 64, 4:8, :],
                    w_o[l, 512 + 256 * w:512 + 256 * w + 256, :].rearrange("(i d) n -> d i n", d=64))
            ugv = ug.rearrange("(a p) c -> a (p c)", p=128)
            def s1_load(sc):
                for j in range(4):
                    dma("sp", Ua[sc % 2][32 * j:32 * j + 32, :],
                        ugv[:, 16384 * j + 2048 * sc:16384 * j + 2048 * (sc + 1)])

            s1_load(0)
            for sc in range(8):
                if sc + 1 < 8:
                    s1_load(sc + 1)
                ysb = Ysb[sc % 4]
                for m in range(4):
                    bk = banks[(sc % 2) * 4 + m]
                    mm(bk[:], wbd[:, :], Ua[sc % 2][:, m * 512:(m + 1) * 512], True, True)
                    cp("act" if m % 2 == 0 else "dve", ysb[:, m * 512:(m + 1) * 512], bk[:])
                for j in range(4):
                    c_ = 16384 * j + 2048 * sc
                    dma("act", ys[:, c_:c_ + 2048], ysb[32 * j:32 * j + 32, :])

            mcs = c16(12288, [2, 4, 512])
            for ri in range(2):
                for cch in range(4):
                    bk = banks[4 + (ri * 4 + cch) % 4]
                    for c2 in range(4):
                        mm(bk[:], ccs[:, ri, c2, cch * 128:(cch + 1) * 128], wf[:, c2, :], c2 == 0, c2 == 3)
                    cp("act" if cch % 2 == 0 else "dve", mcs[:, ri, cch, :], bk[:])

            def stage2(g, ccs_=None):
                yl = Yl[g % 2]
                tl = Tl[g % 2]
                if ccs_ is None or 0 in ccs_:
                    if g < 4:
                        for ri in range(2):
                            r0_ = ri * 16 + 4 * g
                            dma("sp", yl[:, :, ri, :], ys[r0_:r0_ + 4, :].rearrange("j (p c) -> p j c", p=128))
                    else:
                        for j in range(4):
                            ka_ = 4 * g + j
                            rrow = ka_ if ka_ <= 16 else 32 - ka_
                            irow = 48 - ka_ if ka_ >= 17 else 17
                            dma("sp", yl[:, j, 0, :], ys[rrow:rrow + 1, :].rearrange("o (p c) -> p (o c)", p=128))
                            dma("sp", yl[:, j, 1, :], ys[irow:irow + 1, :].rearrange("o (p c) -> p (o c)", p=128))
                    dma("sp", tl, t2_d[g])
                for cc in (range(4) if ccs_ is None else ccs_):
                    for j in range(4):
                        o_ = banks[cc][:, j * 128:(j + 1) * 128]
                        mm(o_, yl[:, j, 0, cc * 128:(cc + 1) * 128], tl[:, j, 0, :], True, False)
                        mm(o_, yl[:, j, 1, cc * 128:(cc + 1) * 128], tl[:, j, 1, :], False, True)

            def evac_group():
                for cc in range(4):
                    bv = banks[cc][:].rearrange("p (j x) -> p j x", j=4, x=128)
                    e_ = "act" if cc % 2 == 0 else "dve"
                    cp(e_, PQ[:, 0, cc, :].rearrange("p (j x) -> p j x", j=4, x=64), bv[:, :, 0:64])
                    cp(e_, PQ[:, 1, cc, :].rearrange("p (j x) -> p j x", j=4, x=64), bv[:, :, 64:128])

            mixFb = [mixF, Zt]

            def wo_part(g, gi_, dcs):
                mf = mixFb[gi_ % 2]
                for dc in dcs:
                    ob = banks[6 + (dc % 2)][:, KG:2 * KG]
                    for i in range(4):
                        mm(ob, wo[:, i, dc * 128:(dc + 1) * 128], mf[:, i, :], i == 0, False)
                    for i in range(4):
                        mm(ob, wo[:, 4 + i, dc * 128:(dc + 1) * 128], mixA[:, i, g * KG:(g + 1) * KG], False, i == 3)
                    xv = xT[:, dc, :].rearrange("p (kpl ka) -> p ka kpl", ka=32)[:, 4 * g:4 * g + 4, :]
                    tt("dve", xv, ob.rearrange("p (j x) -> p j x", j=4, x=64), xv, ALU.add)

            def post_group(g, gi_, nxt=None, prev=None):
                def fill(ccs2):
                    if nxt is not None:
                        stage2(nxt, ccs2)
                if prev is not None:
                    wo_part(prev, gi_ - 1, range(0, 4))
                for c3 in range(4):
                    yb = banks[4 + (c3 % 2)][:, 0:KG]
                    for cc in range(4):
                        mm(yb, mcs[:, 0, cc, c3 * 128:(c3 + 1) * 128], PQ[:, 0, cc, :], cc == 0, False)
                        mm(yb, mcs[:, 1, cc, c3 * 128:(c3 + 1) * 128], PQ[:, 1, cc, :], False, cc == 3)
                    act(yF[:, c3, :], yb, AF.Identity, bias=bF[:, l, c3:c3 + 1])
                fill([0, 1])
                for c3 in range(4):
                    act(sqf[c3 % 2], yF[:, c3, :], AF.Square)
                    mm(banks[5][:, KG:2 * KG], o512, sqf[c3 % 2], c3 == 0, c3 == 3)
                if prev is not None:
                    wo_part(prev, gi_ - 1, range(4, 8))
                rsqrt_eps(rF, banks[5][:, KG:2 * KG])
                mf = mixFb[gi_ % 2]
                for c3 in range(4):
                    stt("dve", mf[:, c3, :], yF[:, c3, :], gF[:, l, c3:c3 + 1], rF, ALU.mult, ALU.mult)
                fill([2, 3])

            sqh = c16(37392, [8, 2])
            hh = c16(37408, [8, 2])
            hstage = c16(37440, [8, 2])
            rh = c32(1280, [2])

            def early_halo():
                for kc in range(NKC):
                    act(sqh[:, kc, :], xT[:, kc, 0:2048:2047], AF.Square)
                    mm(banks[5][:, 0:2], o1024, sqh[:, kc, :], kc == 0, kc == NKC - 1)
                rsqrt_eps(rh, banks[5][:, 0:2])
                for kc in range(NKC):
                    stt("dve", hh[:, kc, :], xT[:, kc, 0:2048:2047], g2[:, l, kc:kc + 1], rh, ALU.mult, ALU.mult)
                dma("pool", hb[0:1, :].rearrange("o (c p) -> p (o c)", p=128), hh[:, :, 0])
                dma("pool", hb[1:2, :].rearrange("o (c p) -> p (o c)", p=128), hh[:, :, 1])
                allgather(hg, hb)
                dma("pool", hstage[:, :, 0], hg[1:2, :].rearrange("o (c p) -> p (o c)", p=128))
                dma("pool", hstage[:, :, 1], hg[2:3, :].rearrange("o (c p) -> p (o c)", p=128))

            pairs = [(grp, fh, jp) for grp in range(2) for fh in range(2) for jp in range(11)]
            wup = [c16(16400 + 2048 * i, [8, 256]) for i in range(3)]

            def load_wup(q):
                grp, fh, jp = pairs[q]
                fc = fh * 11 + jp
                wu = wup[q % 3]
                for wh in range(2):
                    col = wh * DFF + fc * 128
                    dma("pool", wu[:, :, wh * 128:(wh + 1) * 128],
                        w_up[l, :, col:col + 128].rearrange("(c p) n -> p c n", p=128))

            gorder = [7, 0, 1, 2, 3, 4, 5, 6]
            stage2(gorder[0])
            for gi, kg in enumerate(gorder):
                evac_group()
                post_group(kg, gi, gorder[gi + 1] if gi + 1 < NKG else None, gorder[gi - 1] if gi >= 1 else None)
                if gi == 2:
                    early_halo()
                    load_wup(0)
                    load_wup(1)
            wo_part(gorder[-1], NKG - 1, range(0, 8))

            hT2 = c16(0, [8, 2050])
            actT = c16(22544, [11, 1024])
            wdn = c16(33808, [11, 1024])
            wup = [c16(16400 + 2048 * i, [8, 256]) for i in range(3)]
            sq2 = [c16(45072, [512]), c16(45584, [512])]
            sg = c16(46096, [1024])
            xs = [[c32(0, [1026]), c32(1026, [1026])], [c32(2052, [1026]), c32(3078, [1026])]]
            cv = [c32(4104, [1024]), c32(5128, [1024])]
            rstd2 = c32(6152, [504]) if False else None

            for g in range(4):
                rmsnorm_T(l, g, g2, lambda kc: hT2[:, kc, 1 + g * 512:1 + (g + 1) * 512], sq2, cv[0][:, 0:512],
                          banks[g % 2])
            ts("dve", hT2[:, :, 0], hstage[:, :, 0], hmask[:, 0:1], None, ALU.mult)
            ts("dve", hT2[:, :, 2049], hstage[:, :, 1], hmask[:, 1:2], None, ALU.mult)

            for q, (grp, fh, jp) in enumerate(pairs):
                c0 = 1 + 1024 * grp
                if q + 2 < len(pairs):
                    load_wup(q + 2)
                if jp == 0:
                    dma("pool", wdn, w_down[l, fh * 1408:(fh + 1) * 1408, :].rearrange("(j p) n -> p j n", p=128))
                fc = fh * 11 + jp
                wu = wup[q % 3]
                for wh in range(2):
                    ccol = wh * 22 + fc
                    xb = xs[q % 2][wh]
                    ub_ = [banks[2 * wh], banks[2 * wh + 1]]
                    hbk = banks[4 + wh]
                    for hf in range(2):
                        for kc in range(NKC):
                            mm(ub_[hf][:], wu[:, kc, wh * 128:(wh + 1) * 128],
                               hT2[:, kc, c0 + hf * 512:c0 + (hf + 1) * 512], kc == 0, kc == NKC - 1)
                    for kc in range(NKC):
                        mm(hbk[:, 0:2], wu[:, kc, wh * 128:(wh + 1) * 128],
                           hT2[:, kc, c0 - 1:c0 + 1025:1025], kc == 0, kc == NKC - 1)
                    for hf in range(2):
                        cp("act", xb[:, 1 + hf * 512:513 + hf * 512], ub_[hf][:])
                        act(cv[wh][:, hf * 512:(hf + 1) * 512], ub_[hf][:], AF.Identity,
                            bias=cb[:, l, ccol:ccol + 1], scale=cw[:, l, 1, ccol:ccol + 1])
                    cp("act", xb[:, 0:1026:1025], hbk[:, 0:2])
                    stt("dve", cv[wh], xb[:, 0:1024], cw[:, l, 0, ccol:ccol + 1], cv[wh], ALU.mult, ALU.add)
                    stt("dve", cv[wh], xb[:, 2:1026], cw[:, l, 2, ccol:ccol + 1], cv[wh], ALU.mult, ALU.add)
                act(sg, cv[0], AF.Silu)
                tt("pool", actT[:, jp, :], sg, cv[1], ALU.mult)
                if jp == 10:
                    last = (l == depth - 1 and grp == 1 and fh == 1)
                    if not last:
                        for dc in range(NKC):
                            for hf in range(2):
                                db = banks[6 + ((dc * 2 + hf) % 2)]
                                for j2 in range(11):
                                    mm(db[:], wdn[:, j2, dc * 128:(dc + 1) * 128],
                                       actT[:, j2, hf * 512:(hf + 1) * 512], j2 == 0, j2 == 10)
                                tk = slice(grp * 1024 + hf * 512, grp * 1024 + (hf + 1) * 512)
                                tt("dve", xT[:, dc, tk], db[:], xT[:, dc, tk], ALU.add)
                    else:
                        pend = list(range(8))
                        for hf in range(2):
                            for dc in range(NKC):
                                db = banks[6 + (dc % 2)]
                                for j2 in range(11):
                                    mm(db[:], wdn[:, j2, dc * 128:(dc + 1) * 128],
                                       actT[:, j2, hf * 512:(hf + 1) * 512], j2 == 0, j2 == 10)
                                tk = slice(grp * 1024 + hf * 512, grp * 1024 + (hf + 1) * 512)
                                tt("dve", xT[:, dc, tk], db[:], xT[:, dc, tk], ALU.add)
                                if pend and (hf == 1 or dc % 2 == 1):
                                    writeback(pend.pop(0))
                            if hf == 0:
                                pend += [8, 9, 10, 11]

        for t in range(16):
            if t not in wb_done:
                writeback(t)

        P.emit(sem_c)
    return nc


def _consts(h):
    bf = ml_dtypes.bfloat16
    a = np.arange(32, dtype=np.float64)
    ang1 = 2.0 * np.pi * (a[:, None] * a[None, :]) / 32.0
    w32 = np.concatenate([np.cos(ang1), np.sin(ang1)], axis=1)
    w32r = np.concatenate([np.cos(ang1)[:, 0:17], np.sin(ang1)[:, 1:16]], axis=1)
    wbd = np.zeros((4, 32, 4, 32), np.float64)
    for j in range(4):
        wbd[j, :, j, :] = w32r
    wbd = wbd.reshape(128, 128).astype(bf)
    p = np.arange(128, dtype=np.int64)
    ka = np.arange(32, dtype=np.int64)
    kpl = np.arange(64, dtype=np.int64)
    kk = ka[:, None] + 32 * (kpl[None, :] + 64 * h)
    th = 2.0 * np.pi * ((p[:, None, None] * kk[None, :, :]) % SEQ).astype(np.float64) / SEQ
    T1 = np.concatenate([np.cos(th), np.sin(th)], axis=2)
    T2 = np.concatenate([-np.sin(th), np.cos(th)], axis=2)
    sgn = np.where(ka > 16, -1.0, 1.0) * np.where((ka == 0) | (ka == 16), 0.0, 1.0)
    T2 = T2 * sgn[None, :, None]
    t2 = np.stack([T1, T2], axis=2)
    t2 = t2.reshape(128, NKG, 4, 2, 128).transpose(1, 0, 2, 3, 4)
    t2 = np.ascontiguousarray(t2).astype(bf)
    c = np.arange(512, dtype=np.int64)
    a2 = 2.0 * np.pi * ((c[:, None] * c[None, :]) % 512).astype(np.float64) / 512
    scale = 1.0 / np.sqrt(float(SEQ) * 512.0)
    cc = np.stack([np.cos(a2) * scale, -np.sin(a2) * scale], axis=0)
    ccs = cc.reshape(2, 4, 128, 512).transpose(2, 0, 1, 3)
    ccs = np.ascontiguousarray(ccs).astype(bf)
    inv = 1.0 / (10000.0 ** (np.arange(0, 64, 2, dtype=np.float32) / 64.0))
    pos = (2048 * h + np.arange(T)).astype(np.float32)
    angr = pos[None, :] * inv[:, None].astype(np.float32)
    idx = (np.arange(128) % 64) % 32
    rope = np.stack([np.cos(angr)[idx], np.sin(angr)[idx]], axis=1)
    rope = np.ascontiguousarray(rope).astype(bf)
    mats = np.zeros((128, 5, 128), np.float32)
    for m in range(128):
        d = m % 64
        if d < 32:
            mats[m + 32, 0, m] = -1.0
        else:
            mats[m - 32, 0, m] = 1.0
    mats[:64, 1, :64] = 1.0 / 64
    mats[64:, 1, 64:] = 1.0 / 64
    mats[:, 2, :] = 1.0 / 1024
    mats[:, 3, :] = 1.0 / 512
    mats[:, 4, :] = np.eye(128)
    jj = np.arange(128)[:, None]
    qi = np.arange(128)[None, :]
    prev = (jj >= qi).astype(np.float32)
    nxt = (jj <= qi).astype(np.float32)
    zero = np.zeros_like(prev)
    mk = np.stack([prev, nxt, prev if h == 1 else zero, nxt if h == 0 else zero], axis=1)
    masks = np.tile(mk[:, :, None, :], (1, 1, 4, 1)).reshape(128, 4, 512)
    onesz = np.zeros((128, 2, 128), np.float32)
    onesz[:, 0, :64] = 1.0
    onesz[:, 1, 64:] = 1.0
    hmask = np.zeros((128, 2), np.float32)
    hmask[:, 0] = 1.0 if h == 1 else 0.0
    hmask[:, 1] = 1.0 if h == 0 else 0.0
    return {
        "c_wbd": wbd, "c_t2": t2, "c_ccs": ccs, "c_rope": rope, "c_mats": mats.astype(bf),
        "c_identf": np.eye(128, dtype=np.float32), "c_masks": masks.astype(bf),
        "c_onesz": onesz.astype(bf), "c_hmask": hmask,
    }


_NC_CACHE = {}


def kernel(x, norm1, w_in, w_fourier, b_fourier, q_norm, k_norm, sink, g_fourier_out, g_attn_out, w_o,
           norm2, w_up, conv_w, conv_b, w_down):
    f = lambda a: np.ascontiguousarray(np.asarray(a, dtype=np.float32))
    x = f(x)
    shared = {
        "norm1": f(norm1), "norm2": f(norm2), "w_in": f(w_in), "w_fourier": f(w_fourier),
        "b_fourier": f(b_fourier), "q_norm": f(q_norm), "k_norm": f(k_norm), "sink": f(sink),
        "g_fourier_out": f(g_fourier_out), "g_attn_out": f(g_attn_out), "w_o": f(w_o),
        "w_up": f(w_up), "conv_w": f(conv_w), "conv_b": f(conv_b), "w_down": f(w_down),
    }
    consts = [_consts(0), _consts(1)]
    if "nc" not in _NC_CACHE:
        _NC_CACHE["nc"] = build_nc()
    nc = _NC_CACHE["nc"]
    in_maps = []
    for c in range(8):
        b, h = c // 2, c % 2
        m = dict(shared)
        m.update(consts[h])
        m["x"] = np.ascontiguousarray(x[b, h * T:(h + 1) * T, :])
        in_maps.append(m)
    res = run_bass_kernel_spmd(nc, in_maps, core_ids=list(range(8)))
    out = np.empty((4, SEQ, D), np.float32)
    for c in range(8):
        b, h = c // 2, c % 2
        out[b, h * T:(h + 1) * T, :] = res.results[c]["y"]
    return out
```
